# Optimizing a Trainium2 kernel written in Bass

```python
import math
import jax, jax.numpy as jnp
from jax import lax
import numpy as np

D_MODEL = 1024
BATCH = 1
SEQ = 16384
DEPTH = 4

N_MIXERS = 3
HEAD_DIM = 64
MIX_HEADS = 12
MIX_WIDTH = MIX_HEADS * HEAD_DIM
MEM_HEADS = 4
MEM_WIDTH = MEM_HEADS * HEAD_DIM
MEM_LEN = 256
D_FF = 2816
ROT_DIM = HEAD_DIM // 4
ROPE_THETA = 500000.0
Q_BLOCK = 128
CONV_WIDTH = 3
NSA_KV_HEADS = 4
NSA_GROUP = MIX_HEADS // NSA_KV_HEADS
NSA_KV_WIDTH = NSA_KV_HEADS * HEAD_DIM
CMP_LEN = 32
CMP_STRIDE = 16
SLC_LEN = 64
SLC_TOPN = 16
WINDOW = 512
EPS = 1e-6
N_LAYERS_FOX = len(range(0, DEPTH, N_MIXERS))
N_LAYERS_CONV = len(range(1, DEPTH, N_MIXERS))
N_LAYERS_NSA = len(range(2, DEPTH, N_MIXERS))
FOX_IN = 3 * MIX_WIDTH + MIX_HEADS + MEM_WIDTH
CONV_IN = 3 * MIX_WIDTH + MEM_WIDTH
NSA_IN = MIX_WIDTH + 6 * NSA_KV_WIDTH + 3 * MIX_HEADS + MEM_WIDTH

kernel_name = "hybrid_fox_shortconv_nsa_trunk"


def rmsnorm(x, g):
    xf = x.astype(jnp.float32)
    y = xf * lax.rsqrt(jnp.mean(xf * xf, axis=-1, keepdims=True) + EPS)
    return (y * g.astype(jnp.float32)).astype(x.dtype)


def swiglu(h, w_in, w_out):
    g, u = jnp.split(h @ w_in, 2, axis=-1)
    return (jax.nn.silu(g) * u) @ w_out


def split_cols(t, sizes):
    out, off = [], 0
    for s in sizes:
        out.append(t[..., off:off + s])
        off += s
    return out


def partial_rope(x, pos):
    half = ROT_DIM // 2
    inv = ROPE_THETA ** (-jnp.arange(half, dtype=jnp.float32) / half)
    ang = pos.astype(jnp.float32)[:, None] * inv[None, :]
    cos = jnp.cos(ang)[None, :, None, :].astype(x.dtype)
    sin = jnp.sin(ang)[None, :, None, :].astype(x.dtype)
    x1, x2 = x[..., :half], x[..., half:ROT_DIM]
    return jnp.concatenate([x1 * cos - x2 * sin, x2 * cos + x1 * sin, x[..., ROT_DIM:]], axis=-1)


def masked_softmax(s, mask):
    s = jnp.where(mask, s.astype(jnp.float32), -jnp.inf)
    m = jnp.max(s, axis=-1, keepdims=True)
    m = jnp.where(jnp.isfinite(m), m, 0.0)
    p = jnp.exp(s - m)
    return p / jnp.maximum(jnp.sum(p, axis=-1, keepdims=True), jnp.finfo(jnp.float32).tiny)


def fox_attention(q, k, v, f_logit, b_f):
    B, S, H, dh = q.shape
    nb = S // Q_BLOCK
    logf = jax.nn.log_sigmoid(f_logit.astype(jnp.float32) + b_f.astype(jnp.float32))
    F = jnp.cumsum(logf, axis=1).transpose(0, 2, 1)
    qh = q.transpose(0, 2, 1, 3) * (1.0 / math.sqrt(dh))
    kh = k.transpose(0, 2, 1, 3)
    vh = v.transpose(0, 2, 1, 3)
    q_blk = qh.reshape(B, H, nb, Q_BLOCK, dh).transpose(2, 0, 1, 3, 4)
    f_blk = F.reshape(B, H, nb, Q_BLOCK).transpose(2, 0, 1, 3)
    starts = jnp.arange(nb) * Q_BLOCK
    kpos = jnp.arange(S)

    def block(args):
        qb, fb, s0 = args
        tpos = s0 + jnp.arange(Q_BLOCK)
        s = jnp.einsum('bhtd,bhsd->bhts', qb, kh).astype(jnp.float32) + fb[..., None] - F[:, :, None, :]
        p = masked_softmax(s, kpos[None, :] <= tpos[:, None])
        return jnp.einsum('bhts,bhsd->bhtd', p.astype(vh.dtype), vh)

    o = lax.map(block, (q_blk, f_blk, starts))
    return o.transpose(1, 0, 3, 2, 4).reshape(B, S, H * dh)


def short_conv(b_gate, c_gate, v, w):
    u = c_gate * v
    y = lax.conv_general_dilated(u, w[:, None, :], window_strides=(1,), padding=[(CONV_WIDTH - 1, 0)],
                                 dimension_numbers=('NWC', 'WIO', 'NWC'), feature_group_count=u.shape[-1])
    return b_gate * y


def nsa_attention(q, kc, vc, ks, vs, kw, vw, gate_logit, cmp_pos, cmp_w1, cmp_w2, pos):
    B, S, H, dh = q.shape
    G = NSA_KV_HEADS
    dt = q.dtype
    nb = S // Q_BLOCK
    q = partial_rope(q, pos)
    kc = partial_rope(kc, pos)
    ks = partial_rope(ks, pos)
    kw = partial_rope(kw, pos)
    n_cmp = (S - CMP_LEN) // CMP_STRIDE + 1
    cmp_start = jnp.arange(n_cmp) * CMP_STRIDE
    widx = cmp_start[:, None] + jnp.arange(CMP_LEN)[None, :]

    def compress(t, pe, w1, w2):
        blk = t[:, widx] + pe[None, None, :, None, :]
        blk = blk.transpose(0, 1, 3, 2, 4).reshape(B, n_cmp, G, CMP_LEN * dh)
        return (jax.nn.silu(blk @ w1) @ w2).transpose(0, 2, 1, 3)

    k_cmp = compress(kc, cmp_pos[0], cmp_w1[0], cmp_w2[0])
    v_cmp = compress(vc, cmp_pos[1], cmp_w1[1], cmp_w2[1])
    cmp_end = cmp_start + CMP_LEN - 1
    n_slc = S // SLC_LEN
    n_top = min(SLC_TOPN, n_slc)
    slc_start = jnp.arange(n_slc) * SLC_LEN
    ks_blk = ks.transpose(0, 2, 1, 3).reshape(B, G, n_slc, SLC_LEN, dh)
    vs_blk = vs.transpose(0, 2, 1, 3).reshape(B, G, n_slc, SLC_LEN, dh)
    overlap = ((cmp_start[:, None] < slc_start[None, :] + SLC_LEN) &
               (cmp_start[:, None] + CMP_LEN > slc_start[None, :])).astype(jnp.float32)
    kw_pad = jnp.pad(kw.transpose(0, 2, 1, 3), ((0, 0), (0, 0), (WINDOW, 0), (0, 0)))
    vw_pad = jnp.pad(vw.transpose(0, 2, 1, 3), ((0, 0), (0, 0), (WINDOW, 0), (0, 0)))
    qg = (q * (1.0 / math.sqrt(dh))).reshape(B, S, G, NSA_GROUP, dh).transpose(0, 2, 3, 1, 4)
    q_blk = qg.reshape(B, G, NSA_GROUP, nb, Q_BLOCK, dh).transpose(3, 0, 1, 2, 4, 5)
    starts = jnp.arange(nb) * Q_BLOCK
    bi = jnp.arange(B)[:, None, None, None]
    gi = jnp.arange(G)[None, :, None, None]
    jblk = jnp.arange(n_slc)

    def block(args):
        qb, s0 = args
        tpos = s0 + jnp.arange(Q_BLOCK)
        sc = jnp.einsum('bgqtd,bgnd->bgqtn', qb, k_cmp)
        pc = masked_softmax(sc, cmp_end[None, :] <= tpos[:, None])
        oc = jnp.einsum('bgqtn,bgnd->bgqtd', pc.astype(dt), v_cmp)
        imp = jnp.einsum('bgqtn,nj->bgtj', pc, overlap)
        tblk = tpos // SLC_LEN
        valid = jblk[None, :] <= tblk[:, None]
        forced = (jblk[None, :] == 0) | (jblk[None, :] == tblk[:, None]) | (jblk[None, :] == tblk[:, None] - 1)
        score = jnp.where(valid, jnp.where(forced, jnp.inf, imp), -jnp.inf)
        top_val, top_idx = lax.top_k(score, n_top)
        kg = ks_blk[bi, gi, top_idx].reshape(B, G, Q_BLOCK, n_top * SLC_LEN, dh)
        vg = vs_blk[bi, gi, top_idx].reshape(B, G, Q_BLOCK, n_top * SLC_LEN, dh)
        kpos_s = (top_idx[..., None] * SLC_LEN + jnp.arange(SLC_LEN)).reshape(B, G, Q_BLOCK, n_top * SLC_LEN)
        ok = jnp.broadcast_to((top_val > -jnp.inf)[..., None], top_idx.shape + (SLC_LEN,)).reshape(kpos_s.shape)
        mask_s = (kpos_s <= tpos[None, None, :, None]) & ok
        ss = jnp.einsum('bgqtd,bgtnd->bgqtn', qb, kg)
        ps = masked_softmax(ss, mask_s[:, :, None])
        o_s = jnp.einsum('bgqtn,bgtnd->bgqtd', ps.astype(dt), vg)
        kwin = lax.dynamic_slice_in_dim(kw_pad, s0, WINDOW + Q_BLOCK, axis=2)
        vwin = lax.dynamic_slice_in_dim(vw_pad, s0, WINDOW + Q_BLOCK, axis=2)
        wpos = s0 - WINDOW + jnp.arange(WINDOW + Q_BLOCK)
        mask_w = (wpos[None, :] >= 0) & (wpos[None, :] <= tpos[:, None]) & (tpos[:, None] - wpos[None, :] < WINDOW)
        sw = jnp.einsum('bgqtd,bgnd->bgqtn', qb, kwin)
        pw = masked_softmax(sw, mask_w)
        o_w = jnp.einsum('bgqtn,bgnd->bgqtd', pw.astype(dt), vwin)
        return oc, o_s, o_w

    oc, o_s, o_w = lax.map(block, (q_blk, starts))

    def unblock(o):
        return o.transpose(1, 0, 4, 2, 3, 5).reshape(B, S, H, dh)

    g = jax.nn.sigmoid(gate_logit.astype(jnp.float32)).astype(dt)[..., None]
    o = g[:, :, 0] * unblock(oc) + g[:, :, 1] * unblock(o_s) + g[:, :, 2] * unblock(o_w)
    return o.reshape(B, S, H * dh)


def memory_cross(qx, mem_k, mem_v):
    B, S, _ = qx.shape
    q = qx.reshape(B, S, MEM_HEADS, HEAD_DIM)
    s = jnp.einsum('bshd,bmhd->bhsm', q, mem_k) * (1.0 / math.sqrt(HEAD_DIM))
    p = jax.nn.softmax(s.astype(jnp.float32), axis=-1).astype(mem_v.dtype)
    return jnp.einsum('bhsm,bmhd->bshd', p, mem_v).reshape(B, S, MEM_WIDTH)


def setup_inputs(seed: int = 0) -> dict:
    key = jax.random.key(seed)
    ks = jax.random.split(key, 24)

    def nrm(k, shape, scale):
        return jax.random.normal(k, shape, jnp.float32) * scale

    def gain(k, shape):
        return 1.0 + 0.02 * jax.random.normal(k, shape, jnp.float32)

    D = D_MODEL
    return {
        "x": nrm(ks[0], (BATCH, SEQ, D), 1.0),
        "mem": nrm(ks[1], (BATCH, MEM_LEN, D), 1.0),
        "ffn1_norm": gain(ks[2], (DEPTH, D)),
        "ffn1_w_in": nrm(ks[3], (DEPTH, D, 2 * D_FF), D ** -0.5),
        "ffn1_w_out": nrm(ks[4], (DEPTH, D_FF, D), D_FF ** -0.5),
        "mix_norm": gain(ks[5], (DEPTH, D)),
        "mix_w_out": nrm(ks[6], (DEPTH, MIX_WIDTH + MEM_WIDTH, D), (MIX_WIDTH + MEM_WIDTH) ** -0.5),
        "mem_norm": gain(ks[7], (D,)),
        "mem_w_kv": nrm(ks[8], (DEPTH, D, 2 * MEM_WIDTH), D ** -0.5),
        "fox_w_in": nrm(ks[9], (N_LAYERS_FOX, D, FOX_IN), D ** -0.5),
        "fox_b_f": 2.0 + 0.5 * jax.random.normal(ks[10], (N_LAYERS_FOX, MIX_HEADS), jnp.float32),
        "conv_w_in": nrm(ks[11], (N_LAYERS_CONV, D, CONV_IN), D ** -0.5),
        "conv_w": nrm(ks[12], (N_LAYERS_CONV, CONV_WIDTH, MIX_WIDTH), CONV_WIDTH ** -0.5),
        "nsa_w_in": nrm(ks[13], (N_LAYERS_NSA, D, NSA_IN), D ** -0.5),
        "nsa_cmp_pos": nrm(ks[14], (N_LAYERS_NSA, 2, CMP_LEN, HEAD_DIM), 0.1),
        "nsa_cmp_w1": nrm(ks[15], (N_LAYERS_NSA, 2, CMP_LEN * HEAD_DIM, HEAD_DIM), (CMP_LEN * HEAD_DIM) ** -0.5),
        "nsa_cmp_w2": nrm(ks[16], (N_LAYERS_NSA, 2, HEAD_DIM, HEAD_DIM), HEAD_DIM ** -0.5),
        "ffn2_norm": gain(ks[17], (DEPTH, D)),
        "ffn2_w_in": nrm(ks[18], (DEPTH, D, 2 * D_FF), D ** -0.5),
        "ffn2_w_out": nrm(ks[19], (DEPTH, D_FF, D), D_FF ** -0.5),
        "final_norm": gain(ks[20], (D,)),
    }


def reference(x, mem, ffn1_norm, ffn1_w_in, ffn1_w_out, mix_norm, mix_w_out, mem_norm, mem_w_kv,
              fox_w_in, fox_b_f, conv_w_in, conv_w, nsa_w_in, nsa_cmp_pos, nsa_cmp_w1, nsa_cmp_w2,
              ffn2_norm, ffn2_w_in, ffn2_w_out, final_norm):
    B, S, _ = x.shape
    M = mem.shape[1]
    pos = jnp.arange(S)
    mem_n = rmsnorm(mem, mem_norm)
    for i in range(DEPTH):
        x = x + 0.5 * swiglu(rmsnorm(x, ffn1_norm[i]), ffn1_w_in[i], ffn1_w_out[i])
        h = rmsnorm(x, mix_norm[i])
        kind, j = i % N_MIXERS, i // N_MIXERS
        if kind == 0:
            q, k, v, f, qx = split_cols(h @ fox_w_in[j], [MIX_WIDTH, MIX_WIDTH, MIX_WIDTH, MIX_HEADS, MEM_WIDTH])
            hd = (B, S, MIX_HEADS, HEAD_DIM)
            y_mix = fox_attention(q.reshape(hd), k.reshape(hd), v.reshape(hd), f, fox_b_f[j])
        elif kind == 1:
            bg, cg, v, qx = split_cols(h @ conv_w_in[j], [MIX_WIDTH, MIX_WIDTH, MIX_WIDTH, MEM_WIDTH])
            y_mix = short_conv(bg, cg, v, conv_w[j])
        else:
            q, kc, vc, ks_, vs_, kw, vw, gl, qx = split_cols(
                h @ nsa_w_in[j], [MIX_WIDTH] + [NSA_KV_WIDTH] * 6 + [3 * MIX_HEADS, MEM_WIDTH])
            kvd = (B, S, NSA_KV_HEADS, HEAD_DIM)
            y_mix = nsa_attention(q.reshape(B, S, MIX_HEADS, HEAD_DIM), kc.reshape(kvd), vc.reshape(kvd),
                                  ks_.reshape(kvd), vs_.reshape(kvd), kw.reshape(kvd), vw.reshape(kvd),
                                  gl.reshape(B, S, 3, MIX_HEADS), nsa_cmp_pos[j], nsa_cmp_w1[j],
                                  nsa_cmp_w2[j], pos)
        mem_kv = (mem_n @ mem_w_kv[i]).reshape(B, M, 2, MEM_HEADS, HEAD_DIM)
        y_mem = memory_cross(qx, mem_kv[:, :, 0], mem_kv[:, :, 1])
        x = x + jnp.concatenate([y_mix, y_mem], axis=-1) @ mix_w_out[i]
        x = x + 0.5 * swiglu(rmsnorm(x, ffn2_norm[i]), ffn2_w_in[i], ffn2_w_out[i])
    return rmsnorm(x, final_norm)
```

```python
import numpy as np
import concourse.bass as bass
import concourse.mybir as mybir

F32 = mybir.dt.float32
BF16 = mybir.dt.bfloat16
AF = mybir.ActivationFunctionType
ALU = mybir.AluOpType
AX = mybir.AxisListType


class Prog:
    COMPUTE = ("pe", "act", "dve", "pool")

    def __init__(self, nc):
        self.nc = nc
        self.ops = []

    def op(self, eng, meth, r=(), w=(), **kw):
        self.ops.append(dict(eng=eng, fn=(meth, kw), r=tuple(r), w=tuple(w), dma=False, grp=None))
        return len(self.ops) - 1

    def dma(self, eng, grp, r=(), w=(), **kw):
        if grp is None:
            grp = tuple(w)[0]
        self.ops.append(dict(eng=eng, fn=("dma_start", kw), r=tuple(r), w=tuple(w), dma=True, grp=grp))
        return len(self.ops) - 1

    def finalize(self, final_keys=()):
        nc = self.nc
        ops = self.ops
        if not hasattr(self, "sems"):
            self.sems = {}
            self.semctx = []
            self.cnt = {}
            self.seen = {e: {} for e in ("pe", "act", "dve", "pool", "sp")}
            self.tot_ops = 0
        ops.append(dict(eng="sp", fn=None, r=tuple(final_keys), w=(), dma=False, grp=None, final=True))
        last_w = {}
        last_r = {}
        last_touch = {}
        deps = []
        for i, o in enumerate(ops):
            d = {}
            for k in set(o["r"]) | set(o["w"]):
                if k.startswith("ps"):
                    lt = last_touch.setdefault(k, {})
                    for e2, j in lt.items():
                        if e2 != o["eng"]:
                            d[j] = "excl"
                    lt[o["eng"]] = i
            for k in o["r"]:
                if k in last_w:
                    d[last_w[k]] = "raw"
            for k in o["w"]:
                if k in last_w:
                    d.setdefault(last_w[k], "waw")
                for rr in last_r.get(k, ()):
                    d.setdefault(rr, "war")
            if o.get("final"):
                for j, p in enumerate(ops):
                    if p["dma"]:
                        d.setdefault(j, "raw")
            d.pop(i, None)
            dd = []
            for j, kind in d.items():
                p = ops[j]
                if (not p["dma"]) and (not o["dma"]) and p["eng"] == o["eng"]:
                    if o["eng"] == "pe" or o["eng"] == "sp":
                        continue
                    if kind != "raw":
                        continue
                dd.append(j)
            deps.append(dd)
            for k in o["w"]:
                last_w[k] = i
                last_r[k] = []
            for k in o["r"]:
                last_r.setdefault(k, []).append(i)
        signaled = set()
        for dd in deps:
            signaled.update(dd)

        if not hasattr(self, "dmapool"):
            self.dmapool = []
        grp2phys = {}

        def getsem(name):
            if name not in self.sems:
                cm = nc.semaphore("s%d" % len(self.sems))
                sh = cm.__enter__()
                self.semctx.append(cm)
                self.sems[name] = sh
            return self.sems[name]

        def physname(grp):
            if grp not in grp2phys:
                idx = len(grp2phys)
                grp2phys[grp] = ("dmaphys", idx)
            return grp2phys[grp]

        cnt = self.cnt
        event = {}
        for i, o in enumerate(ops):
            if o["dma"]:
                sname = physname(o["grp"])
                cnt[sname] = cnt.get(sname, 0) + 16
                event[i] = (sname, cnt[sname])
            elif i in signaled:
                sname = ("eng", o["eng"])
                cnt[sname] = cnt.get(sname, 0) + 1
                event[i] = (sname, cnt[sname])
        streams = {e: [] for e in ("pe", "act", "dve", "pool", "sp")}
        seen = self.seen
        nwait = 0
        for i, o in enumerate(ops):
            e = o["eng"]
            need = {}
            for j in deps[i]:
                sname, val = event[j]
                need[sname] = max(need.get(sname, 0), val)
            for sname, val in need.items():
                if seen[e].get(sname, 0) >= val:
                    continue
                seen[e][sname] = val
                streams[e].append(("wait", getsem(sname), val))
                nwait += 1
            if o["fn"] is not None:
                ev = event.get(i)
                if ev is not None:
                    streams[e].append(("op", o["fn"], getsem(ev[0]), 16 if o["dma"] else 1))
                else:
                    streams[e].append(("op", o["fn"], None, 0))
        self.streams = streams
        self.tot_ops += len(ops)
        self.stats = dict(nops=len(ops), tot=self.tot_ops, nwait=nwait, nsem=len(self.sems),
                          per_eng={e: len(st) for e, st in streams.items()},
                          maxcnt=max(cnt.values()) if cnt else 0)
        return streams

    def flush(self, final_keys=()):
        self.finalize(final_keys)
        self.emit()
        self.ops = []

    def close(self):
        for cm in reversed(getattr(self, "semctx", [])):
            cm.__exit__(None, None, None)
        self.semctx = []

    def emit(self):
        nc = self.nc
        streams = self.streams

        def run(engh, items):
            for it in items:
                if it[0] == "wait":
                    engh.wait_ge(it[1], it[2])
                else:
                    ins = getattr(engh, it[1][0])(**it[1][1])
                    if it[2] is not None:
                        ins.then_inc(it[2], it[3])

        with nc.Block() as block:
            @block.sync
            def _(e):
                run(e, streams["sp"])

            @block.tensor
            def _(e):
                run(e, streams["pe"])

            @block.scalar
            def _(e):
                run(e, streams["act"])

            @block.vector
            def _(e):
                run(e, streams["dve"])

            @block.gpsimd
            def _(e):
                run(e, streams["pool"])


import numpy as np
from contextlib import ExitStack
import concourse.bass as bass
import concourse.mybir as mybir

D = 1024
NT = 2048
NTILE = 4
S = 16384
DFF = 2816
NFB = 22
EPS = 1e-6


NEG = -30000.0


def local_tokens(c):
    return np.concatenate([np.arange((8 * j + c) * 128, (8 * j + c + 1) * 128) for j in range(16)])


def host_maskadd(c):
    m = np.zeros((128, 8, 128), np.float32)
    k = np.arange(128)[:, None]
    q = np.arange(128)[None, :]
    for kb in range(8):
        if kb == c:
            m[:, kb, :] = np.where(q >= k, 0.0, NEG)
        elif kb > c:
            m[:, kb, :] = NEG
    return m


def host_selmask(c):
    m = np.zeros((128, 16, 8, 12), np.float32)
    m[:, :, c, :] = 1.0
    return m.reshape(128, 1536)


class KB:
    def __init__(self):
        self.nc = bass.Bass("TRN2", target_bir_lowering=False)
        self.P = Prog(self.nc)
        self.es = ExitStack()
        self.cur = self.es
        self.outs = []
        self.psrr = 0

    def dram_in(self, name, shape, dt=F32):
        return self.nc.dram_tensor(name, list(shape), dt, kind="ExternalInput").ap()

    def dram_out(self, name, shape, dt=F32):
        self.outs.append(name)
        return self.nc.dram_tensor(name, list(shape), dt, kind="ExternalOutput").ap()

    def sb(self, name, shape, dt):
        self.nsb = getattr(self, "nsb", 0) + 1
        return self.cur.enter_context(self.nc.sbuf_tensor(f"{name}_{self.nsb}", list(shape), dt))

    def scope(self):
        kb = self

        class _S:
            def __enter__(s_):
                s_.prev = kb.cur
                s_.st = ExitStack()
                kb.cur = s_.st
                return s_

            def __exit__(s_, *a):
                if a[0] is None:
                    kb.P.flush()
                    print("  flush", kb.P.stats, flush=True)
                s_.st.close()
                kb.cur = s_.prev
                return False
        return _S()

    def psum(self, name, shape, dt=F32):
        return self.es.enter_context(self.nc.psum_tensor(name, list(shape), dt))

    def ring(self, name, n, shape, dt):
        tiles = [self.sb(f"{name}{i}", shape, dt) for i in range(n)]
        st = {"i": 0}

        def nxt():
            i = st["i"] % n
            st["i"] += 1
            return tiles[i], f"{name}{i}"
        return nxt

    def setup_common(self):
        P = self.P
        self.ps = [self.psum(f"ps{i}", [128, 512], F32) for i in range(8)]
        self.psk = [f"ps{i}" for i in range(8)]
        self.ones_bf = self.sb("ones_bf", [128, 128], BF16)
        P.op("pool", "memset", w=["ones_bf"], ap=self.ones_bf[:], constant=1.0)
        self.eps_col = self.sb("eps_col", [128, 1], F32)
        P.op("pool", "memset", w=["eps"], ap=self.eps_col[:], constant=EPS)
        self.xT = [self.sb(f"xT{c}", [128, NT], F32) for c in range(8)]
        self.hT = [self.sb(f"hT{c}", [128, NT], BF16) for c in range(8)]

    def norm_setup(self):
        self.sq = self.ring("sq", 3, [128, 512], BF16)
        self.rstd = self.ring("rstd", 2, [128, 512], F32)
        self.gcol = self.ring("gcol", 2, [128, 8], F32)

    def load_xT(self, x_dram):
        for c in range(8):
            self.P.dma("sp", f"xT{c}", w=[f"x{c}.{t}" for t in range(NTILE)], out=self.xT[c][:], in_=x_dram[c])

    def store_xT(self, x_out):
        for c in range(8):
            self.P.dma("sp", f"xT{c}", r=[f"x{c}.{t}" for t in range(NTILE)], w=[f"xo{c}"], out=x_out[c], in_=self.xT[c][:])
        return [f"xo{c}" for c in range(8)]

    def rmsnorm(self, g_dram, ntile=NTILE, xT=None, hT=None, xk="x", hk="h", width=512, final=False):
        P = self.P
        xT = xT or self.xT
        hT = hT or self.hT
        if final:
            hT, hk = xT, xk
        gt, gk = self.gcol()
        P.dma("sp", gk, w=[gk], out=gt[:], in_=g_dram.rearrange("(c p) -> p c", p=128), allow_slow_non_contiguous=True)
        for t in range(ntile):
            cs = slice(t * width, (t + 1) * width)
            pb, pk = self.ps[7], self.psk[7]
            for c in range(8):
                sq, sqk = self.sq()
                P.op("act", "activation", r=[f"{xk}{c}.{t}"], w=[sqk], out=sq[:, 0:width], in_=xT[c][:, cs], func=AF.Square)
                P.op("pe", "matmul", r=[sqk, "ones_bf"], w=[pk], out=pb[:, 0:width], lhsT=self.ones_bf[:], rhs=sq[:, 0:width],
                     start=(c == 0), stop=(c == 7))
            rs, rk = self.rstd()
            P.op("act", "activation", r=[pk, "eps"], w=[rk], out=rs[:, 0:width], in_=pb[:, 0:width], func=AF.Sqrt,
                 scale=1.0 / D, bias=self.eps_col[:, 0:1])
            P.op("dve", "reciprocal", r=[rk], w=[rk], out=rs[:, 0:width], in_=rs[:, 0:width])
            for c in range(8):
                P.op("dve", "scalar_tensor_tensor", r=[f"{xk}{c}.{t}", gk, rk], w=[f"{hk}{c}.{t}"],
                     out=hT[c][:, cs], in0=xT[c][:, cs], scalar=gt[:, c:c + 1], in1=rs[:, 0:width], op0=ALU.mult, op1=ALU.mult)

    def ffn_setup(self, nb=2):
        self.norm_setup()
        self.ffn_nb = nb
        self.wgu = self.ring("wgu", 3, [128, 8, 256], BF16)
        self.wo = self.ring("wo", 2, [128, nb, D], BF16)
        self.aT = [self.sb(f"aT{j}", [128, NT], BF16) for j in range(nb)]
        self.sg = self.ring("sg", 3, [128, 512], F32)

    def ffn(self, w_in, w_out):
        P = self.P
        nb = self.ffn_nb
        win_v = w_in.rearrange("(c p) n -> p c n", p=128)
        for part in range(NFB // nb):
            for jj in range(nb):
                j = part * nb + jj
                wt, wk = self.wgu()
                P.dma("pool", wk + "g", w=[wk + "g"], out=wt[:, :, 0:128], in_=win_v[:, :, j * 128:(j + 1) * 128])
                P.dma("pool", wk + "u", w=[wk + "u"], out=wt[:, :, 128:256], in_=win_v[:, :, DFF + j * 128:DFF + (j + 1) * 128])
                for t in range(NTILE):
                    cs = slice(t * 512, (t + 1) * 512)
                    ig = (self.psrr % 2) * 2
                    self.psrr += 1
                    pg, pgk = self.ps[ig], self.psk[ig]
                    pu, puk = self.ps[ig + 1], self.psk[ig + 1]
                    for c in range(8):
                        P.op("pe", "matmul", r=[wk + "g", f"h{c}.{t}"], w=[pgk], out=pg[:], lhsT=wt[:, c, 0:128], rhs=self.hT[c][:, cs],
                             start=(c == 0), stop=(c == 7))
                    for c in range(8):
                        P.op("pe", "matmul", r=[wk + "u", f"h{c}.{t}"], w=[puk], out=pu[:], lhsT=wt[:, c, 128:256], rhs=self.hT[c][:, cs],
                             start=(c == 0), stop=(c == 7))
                    sg, sgk = self.sg()
                    P.op("act", "activation", r=[pgk], w=[sgk], out=sg[:], in_=pg[:], func=AF.Silu)
                    P.op("dve", "tensor_tensor", r=[sgk, puk], w=[f"a{jj}.{t}"], out=self.aT[jj][:, cs], in0=sg[:], in1=pu[:], op=ALU.mult)
            wo, wok = self.wo()
            P.dma("pool", wok, w=[wok], out=wo[:],
                  in_=w_out[part * nb * 128:(part + 1) * nb * 128, :].rearrange("(j p) n -> p j n", p=128))
            for c in range(8):
                for t in range(NTILE):
                    cs = slice(t * 512, (t + 1) * 512)
                    ip = 4 + (self.psrr % 2)
                    self.psrr += 1
                    po, pok = self.ps[ip], self.psk[ip]
                    for jj in range(nb):
                        P.op("pe", "matmul", r=[wok, f"a{jj}.{t}"], w=[pok], out=po[:], lhsT=wo[:, jj, c * 128:(c + 1) * 128],
                             rhs=self.aT[jj][:, cs], start=(jj == 0), stop=(jj == nb - 1))
                    P.op("dve", "scalar_tensor_tensor", r=[pok, f"x{c}.{t}"], w=[f"x{c}.{t}"], out=self.xT[c][:, cs], in0=po[:],
                         scalar=0.5, in1=self.xT[c][:, cs], op0=ALU.mult, op1=ALU.add)

    def finish(self, final_keys=()):
        if self.P.ops:
            self.P.flush(final_keys)
            print("  flush", self.P.stats, flush=True)
        self.P.close()
        self.es.close()
        return self.nc


def _proj_setup(self):
    self.norm_setup()
    self.wp = self.ring("wp", 3, [128, 8, 128], BF16)
    self.stg = self.ring("stg", 3, [128, NT], BF16)
    self._stgf = None
    self.one_col = self.sb("one_col", [128, 1], F32)
    self.P.op("pool", "memset", w=["one_col"], ap=self.one_col[:], constant=1.0)


def _proj_fm(self, w_cols, n, evac):
    P = self.P
    wt, wk = self.wp()
    P.dma("pool", wk, w=[wk], out=wt[:, :, 0:n], in_=w_cols.rearrange("(c p) n -> p c n", p=128))
    for t in range(NTILE):
        cs = slice(t * 512, (t + 1) * 512)
        ip = 4 + (self.psrr % 3)
        self.psrr += 1
        pp, ppk = self.ps[ip], self.psk[ip]
        for c in range(8):
            P.op("pe", "matmul", r=[wk, f"h{c}.{t}"], w=[ppk], out=pp[0:n, :], lhsT=wt[:, c, 0:n], rhs=self.hT[c][:, cs],
                 start=(c == 0), stop=(c == 7))
        evac(pp, ppk, t)


def _proj_fm_store(self, w_cols, n, dst, scale=1.0, f32=False):
    P = self.P
    st, sk = (self.stgf() if f32 else self.stg())

    def evac(pp, ppk, t):
        P.op("act", "activation", r=[ppk], w=[f"{sk}.{t}"], out=st[0:n, t * 512:(t + 1) * 512], in_=pp[0:n, :], func=AF.Copy, scale=scale)
    self.proj_fm(w_cols, n, evac)
    P.dma("sp", sk, r=[f"{sk}.{t}" for t in range(NTILE)], w=[f"{sk}.{t}" for t in range(NTILE)] + ["dram." + dst.tensor.name],
          out=dst, in_=st[0:n, :])


def _stgf_get(self):
    if self._stgf is None:
        self._stgf = self.ring("stgf", 2, [128, NT], F32)
    return self._stgf()


KB.stgf = _stgf_get
KB.proj_setup = _proj_setup
KB.proj_fm = _proj_fm
KB.proj_fm_store = _proj_fm_store


def _fox_proj(self, w_in, b_f, qT, kT, v_tm, nlogf, qxT, parts="ab"):
    P = self.P
    if "a" in parts:
        for ch in range(6):
            self.proj_fm_store(w_in[:, ch * 128:(ch + 1) * 128], 128, qT[ch * 128:(ch + 1) * 128, :], scale=0.125)
        for ch in range(6):
            self.proj_fm_store(w_in[:, 768 + ch * 128:768 + (ch + 1) * 128], 128, kT[ch * 128:(ch + 1) * 128, :])
        for ch in range(2):
            self.proj_fm_store(w_in[:, 2316 + ch * 128:2316 + (ch + 1) * 128], 128, qxT[ch * 128:(ch + 1) * 128, :], scale=0.125)
    if "b" not in parts:
        return
    wv = self.sb("wv", [128, 8, 768], BF16)
    P.dma("pool", "wv", w=["wv"], out=wv[:], in_=w_in[:, 1536:2304].rearrange("(c p) n -> p c n", p=128))
    wf32 = self.sb("wf32", [128, 8, 12], F32)
    P.dma("sp", "wf32", w=["wf32"], out=wf32[:], in_=w_in[:, 2304:2316].rearrange("(c p) n -> p c n", p=128))
    wfb = self.sb("wfb", [128, 8, 12], BF16)
    P.op("dve", "tensor_copy", r=["wf32"], w=["wfb"], out=wfb[:], in_=wf32[:])
    bft = self.sb("bft", [128, 12], F32)
    P.dma("sp", "bft", w=["bft"], out=bft[:], in_=b_f.partition_broadcast(128))
    vst = self.ring("vst", 3, [128, 768], BF16)
    nlf = self.sb("nlf", [128, 16, 12], F32)
    zt = self.ring("zt", 2, [128, 12], F32)
    for tb in range(16):
        t = tb // 4
        ts_ = slice(tb * 128, (tb + 1) * 128)
        vs, vk = vst()
        for (c0, n) in ((0, 512), (512, 256)):
            ip = 4 + (self.psrr % 3)
            self.psrr += 1
            pp, ppk = self.ps[ip], self.psk[ip]
            for c in range(8):
                P.op("pe", "matmul", r=["wv", f"h{c}.{t}"], w=[ppk], out=pp[:, 0:n], lhsT=self.hT[c][:, ts_], rhs=wv[:, c, c0:c0 + n],
                     start=(c == 0), stop=(c == 7))
            if c0 == 0:
                P.op("act", "activation", r=[ppk], w=[vk + "a"], out=vs[:, 0:512], in_=pp[:, 0:512], func=AF.Copy)
            else:
                for c in range(8):
                    P.op("pe", "matmul", r=["wfb", f"h{c}.{t}"], w=[ppk], out=pp[:, 256:268], lhsT=self.hT[c][:, ts_], rhs=wfb[:, c, :],
                         start=(c == 0), stop=(c == 7))
                P.op("act", "activation", r=[ppk], w=[vk + "b"], out=vs[:, 512:768], in_=pp[:, 0:256], func=AF.Copy)
                z, zk = zt()
                P.op("dve", "tensor_tensor", r=[ppk, "bft"], w=[zk], out=z[:], in0=pp[:, 256:268], in1=bft[:], op=ALU.add)
                P.op("act", "activation", r=[zk], w=[zk], out=z[:], in_=z[:], func=AF.Exp, scale=-1.0)
                P.op("act", "activation", r=[zk, "one_col"], w=[f"nlf.{tb}"], out=nlf[:, tb, :], in_=z[:], func=AF.Ln, bias=self.one_col[:, 0:1])
        P.dma("sp", vk, r=[vk + "a", vk + "b"], w=[vk + "a", vk + "b", "dram.v"], out=v_tm[ts_, :], in_=vs[:])
    P.dma("sp", "nlf", r=[f"nlf.{tb}" for tb in range(16)], w=["dram.nlf"], out=nlogf.rearrange("(b p) h -> p b h", p=128), in_=nlf[:])


KB.fox_proj = _fox_proj


def _attn_consts(self):
    P = self.P
    self.onesf = self.sb("onesf", [128, 128], F32)
    P.op("pool", "memset", w=["onesf"], ap=self.onesf[:], constant=1.0)
    self.trif = self.sb("trif", [128, 128], F32)
    P.op("pool", "affine_select", r=["onesf"], w=["trif"], out=self.trif[:], in_=self.onesf[:], pattern=[[1, 128]],
         compare_op=ALU.is_ge, fill=0.0, base=0, channel_multiplier=-1)
    self.identf = self.sb("identf", [128, 128], F32)
    P.op("pool", "affine_select", r=["onesf"], w=["identf"], out=self.identf[:], in_=self.onesf[:], pattern=[[1, 128]],
         compare_op=ALU.is_equal, fill=0.0, base=0, channel_multiplier=-1)
    self.ident_bf = self.sb("ident_bf", [128, 128], BF16)
    P.op("pool", "tensor_copy", r=["identf"], w=["ident_bf"], out=self.ident_bf[:], in_=self.identf[:])
    self.pt = self.ring("pt", 6, [128, 512], BF16)
    self.rz = self.ring("rz", 2, [128, 512], F32)


def _fox_attn(self, qT, kT_all, v_all, nlf_all, maskadd_d, selmask_d):
    P = self.P
    NL = self.sb("NL", [128, 128, 12], F32)
    for cp in range(8):
        P.dma("sp", "NL", w=[f"NL.{cp}"], out=NL[:, cp::8, :], in_=nlf_all[cp].rearrange("(j p) h -> p j h", p=128))
    NLf = NL[:].rearrange("p b h -> p (b h)")
    TOT = self.sb("TOT", [128, 128, 12], F32)
    TOTf = TOT[:].rearrange("p b h -> p (b h)")
    CS = self.sb("CS", [128, 128, 12], F32)
    CSf = CS[:].rearrange("p b h -> p (b h)")
    negF = self.sb("negF", [128, 128, 12], F32)
    negFf = negF[:].rearrange("p b h -> p (b h)")
    nlk = [f"NL.{cp}" for cp in range(8)]
    for b in range(3):
        P.op("pe", "matmul", r=nlk + ["onesf"], w=[self.psk[3 + b]], out=self.ps[3 + b][:], lhsT=self.onesf[:], rhs=NLf[:, b * 512:(b + 1) * 512],
             start=True, stop=True)
        P.op("act", "activation", r=[self.psk[3 + b]], w=[f"TOT.{b}"], out=TOTf[:, b * 512:(b + 1) * 512], in_=self.ps[3 + b][:], func=AF.Copy)
    for b in range(3):
        P.op("pe", "matmul", r=nlk + ["trif"], w=[self.psk[b]], out=self.ps[b][:], lhsT=self.trif[:], rhs=NLf[:, b * 512:(b + 1) * 512],
             start=True, stop=True)
    for h in range(12):
        P.op("dve", "tensor_tensor_scan", r=["TOT.0", "TOT.1", "TOT.2", "onesf"], w=[f"CS.{h}"], out=CS[:, :, h], data0=self.onesf[:],
             data1=TOT[:, :, h], initial=0.0, op0=ALU.mult, op1=ALU.add)
    csk = [f"CS.{h}" for h in range(12)]
    P.op("dve", "tensor_copy", r=[self.psk[0]], w=["negF.a"], out=negFf[:, 0:12], in_=self.ps[0][:, 0:12])
    P.op("dve", "tensor_tensor", r=[self.psk[0]] + csk, w=["negF.0"], out=negFf[:, 12:512], in0=self.ps[0][:, 12:512], in1=CSf[:, 0:500], op=ALU.add)
    P.op("dve", "tensor_tensor", r=[self.psk[1]] + csk, w=["negF.1"], out=negFf[:, 512:1024], in0=self.ps[1][:], in1=CSf[:, 500:1012], op=ALU.add)
    P.op("dve", "tensor_tensor", r=[self.psk[2]] + csk, w=["negF.2"], out=negFf[:, 1024:1536], in0=self.ps[2][:], in1=CSf[:, 1012:1524], op=ALU.add)
    nfk = ["negF.a", "negF.0", "negF.1", "negF.2"]
    selm = self.sb("selm", [128, 1536], F32)
    P.dma("sp", "selm", w=["selm"], out=selm[:], in_=selmask_d)
    prod = self.sb("prod", [128, 1536], F32)
    P.op("dve", "tensor_tensor", r=nfk + ["selm"], w=["prod"], out=prod[:], in0=negFf, in1=selm[:], op=ALU.mult)
    Fql = self.sb("Fql", [128, 16, 12], F32)
    P.op("dve", "tensor_reduce", r=["prod"], w=["Fql"], out=Fql[:], in_=prod[:].rearrange("p (j c h) -> p j h c", j=16, c=8, h=12),
         axis=AX.X, op=ALU.add)
    Fq_rows = self.sb("Fq_rows", [12, NT], BF16)
    for T in range(4):
        pb, pk = self.ps[4 + T], self.psk[4 + T]
        for jj in range(4):
            j = 4 * T + jj
            P.op("pe", "transpose", r=["Fql", "identf"], w=[pk], out=pb[0:12, jj * 128:(jj + 1) * 128], in_=Fql[:, j, :], identity=self.identf[:])
        P.op("act", "activation", r=[pk], w=[f"Fq.{T}"], out=Fq_rows[:, T * 512:(T + 1) * 512], in_=pb[0:12, :], func=AF.Copy, scale=-1.0)
    fqk = [f"Fq.{T}" for T in range(4)]
    madd = self.sb("madd", [128, 8, 128], BF16)
    P.dma("pool", "madd", w=["madd"], out=madd[:], in_=maskadd_d.rearrange("p (b q) -> p b q", b=8))
    QA = self.ring("QA", 2, [65, NT], BF16)
    KA = self.ring("KA", 3, [65, 1024], BF16)
    VAe = self.ring("VAe", 2, [128, 8, 128], BF16)
    VAo = self.ring("VAo", 2, [128, 8, 128], BF16)
    ka_slots = [KA() for _ in range(3)]
    for kt, kk in ka_slots:
        P.op("pool", "memset", w=[kk + ".one"], ap=kt[64:65, :], constant=1.0)
    for ring_, lo in ((VAe, 64), (VAo, 0)):
        for _ in range(2):
            vt, vk = ring_()
            P.op("pool", "memset", w=[vk + ".one"], ap=vt[:, :, lo:lo + 64], constant=1.0)
    sti = 0
    for h in range(12):
        even = (h % 2 == 0)
        qa, qk = QA()
        P.dma("sp", qk, w=[qk + ".q"], out=qa[0:64, :], in_=qT[h * 64:(h + 1) * 64, :])
        P.dma("sp", qk, r=fqk, w=[qk + ".f"], out=qa[64:65, :], in_=Fq_rows[h:h + 1, :])
        for J in range(16):
            ka, kk = KA()
            P.dma("sp", kk, w=[kk], out=ka[0:64, :].rearrange("d (c p) -> d c p", c=8),
                  in_=kT_all[:, h * 64:(h + 1) * 64, J * 128:(J + 1) * 128].rearrange("c d p -> d c p"))
            va, vk = (VAe if even else VAo)()
            vlo = 0 if even else 64
            P.dma("sp", vk, w=[vk], out=va[:, :, vlo:vlo + 64],
                  in_=v_all[:, J * 128:(J + 1) * 128, h * 64:(h + 1) * 64].rearrange("c p d -> p c d"))
            for T in range(4):
                jstart = max(J, 4 * T)
                if jstart > 4 * T + 3:
                    continue
                c0 = (jstart - 4 * T) * 128
                ncol = 512 - c0
                diag = (J >= 4 * T)
                acc, acck = self.ps[T], self.psk[T]
                for kb in range(8):
                    si = 4 + (sti % 4)
                    sti += 1
                    sp_, spk = self.ps[si], self.psk[si]
                    P.op("pe", "matmul", r=[kk, kk + ".one", qk + ".q", qk + ".f"], w=[spk], out=sp_[:, 0:ncol], lhsT=ka[:, kb * 128:(kb + 1) * 128],
                         rhs=qa[:, T * 512 + c0:(T + 1) * 512], start=True, stop=(not diag))
                    if diag:
                        P.op("pe", "matmul", r=["ident_bf", "madd"], w=[spk], out=sp_[:, 0:128], lhsT=self.ident_bf[:], rhs=madd[:, kb, :],
                             start=False, stop=True)
                    pt, ptk = self.pt()
                    P.op("act", "activation", r=[spk] + nfk, w=[ptk], out=pt[:, 0:ncol], in_=sp_[:, 0:ncol], func=AF.Exp,
                         bias=negF[:, 8 * J + kb, h:h + 1])
                    first = (J == 0 and kb == 0)
                    last = (J == 4 * T + 3 and kb == 7)
                    P.op("pe", "matmul", r=[ptk, vk, vk + ".one"], w=[acck], out=acc[:, c0:512], lhsT=va[:, kb, :], rhs=pt[:, 0:ncol],
                         start=first, stop=last)
        for T in range(4):
            acc, acck = self.ps[T], self.psk[T]
            rz, rzk = self.rz()
            cs = slice(T * 512, (T + 1) * 512)
            nlo, zlo = (0, 64) if even else (64, 0)
            P.op("dve", "reciprocal", r=[acck], w=[rzk], out=rz[zlo:zlo + 64, :], in_=acc[zlo:zlo + 64, :])
            P.op("dve", "tensor_tensor", r=[acck, rzk], w=[f"h{h // 2}.{T}"], out=self.hT[h // 2][nlo:nlo + 64, cs], in0=acc[nlo:nlo + 64, :],
                 in1=rz[zlo:zlo + 64, :], op=ALU.mult)


KB.attn_consts = _attn_consts
KB.fox_attn = _fox_attn


def _mem_attn(self, memT_d, mem_norm, w_kv, qxT):
    P = self.P
    mT = [self.sb(f"mT{c}", [128, 256], F32) for c in range(8)]
    mhT = [self.sb(f"mhT{c}", [128, 256], BF16) for c in range(8)]
    for c in range(8):
        P.dma("sp", f"mT{c}", w=[f"m{c}.0"], out=mT[c][:], in_=memT_d[c])
    self.rmsnorm(mem_norm, ntile=1, xT=mT, hT=mhT, xk="m", hk="mh", width=256)
    wkv = self.sb("wkv", [128, 8, 512], BF16)
    P.dma("pool", "wkv", w=["wkv"], out=wkv[:], in_=w_kv.rearrange("(c p) n -> p c n", p=128))
    mK = self.sb("mK", [64, 4, 256], BF16)
    mV = [self.sb(f"mV{hm}", [128, 2, 128], BF16) for hm in range(4)]
    mhk = [f"mh{c}.0" for c in range(8)]
    for hm in range(4):
        pb, pk = self.ps[4 + hm], self.psk[4 + hm]
        for c in range(8):
            P.op("pe", "matmul", r=["wkv"] + mhk, w=[pk], out=pb[0:64, 0:256], lhsT=wkv[:, c, hm * 64:(hm + 1) * 64], rhs=mhT[c][:, :],
                 start=(c == 0), stop=(c == 7))
        P.op("act", "activation", r=[pk], w=["mK"], out=mK[:, hm, :], in_=pb[0:64, 0:256], func=AF.Copy)
        lo = 64 if hm % 2 == 0 else 0
        P.op("pool", "memset", w=[f"mV{hm}.one"], ap=mV[hm][:, :, lo:lo + 64], constant=1.0)
    for mb in range(2):
        pb, pk = self.ps[mb], self.psk[mb]
        for c in range(8):
            P.op("pe", "matmul", r=["wkv"] + mhk, w=[pk], out=pb[:, 0:256], lhsT=mhT[c][:, mb * 128:(mb + 1) * 128], rhs=wkv[:, c, 256:512],
                 start=(c == 0), stop=(c == 7))
        for hm in range(4):
            lo = 0 if hm % 2 == 0 else 64
            P.op("act", "activation", r=[pk], w=[f"mV{hm}.v"], out=mV[hm][:, mb, lo:lo + 64], in_=pb[:, hm * 64:(hm + 1) * 64], func=AF.Copy)
    qx = self.sb("qx", [64, 4, NT], BF16)
    P.dma("sp", "qx", w=["qx"], out=qx[:], in_=qxT.rearrange("(h d) n -> d h n", h=4))
    sti = 0
    for hm in range(4):
        even = hm % 2 == 0
        for T in range(4):
            acc, acck = self.ps[T], self.psk[T]
            cs = slice(T * 512, (T + 1) * 512)
            for mb in range(2):
                si = 4 + (sti % 4)
                sti += 1
                sp_, spk = self.ps[si], self.psk[si]
                P.op("pe", "matmul", r=["mK", "qx"], w=[spk], out=sp_[:], lhsT=mK[:, hm, mb * 128:(mb + 1) * 128], rhs=qx[:, hm, cs], start=True, stop=True)
                pt, ptk = self.pt()
                P.op("act", "activation", r=[spk], w=[ptk], out=pt[:], in_=sp_[:], func=AF.Exp)
                P.op("pe", "matmul", r=[ptk, f"mV{hm}.v", f"mV{hm}.one"], w=[acck], out=acc[:], lhsT=mV[hm][:, mb, :], rhs=pt[:], start=(mb == 0), stop=(mb == 1))
            rz, rzk = self.rz()
            nlo, zlo = (0, 64) if even else (64, 0)
            P.op("dve", "reciprocal", r=[acck], w=[rzk], out=rz[zlo:zlo + 64, :], in_=acc[zlo:zlo + 64, :])
            P.op("dve", "tensor_tensor", r=[acck, rzk], w=[f"h{6 + hm // 2}.{T}"], out=self.hT[6 + hm // 2][nlo:nlo + 64, cs], in0=acc[nlo:nlo + 64, :],
                 in1=rz[zlo:zlo + 64, :], op=ALU.mult)


def _mix_out(self, w_o):
    P = self.P
    wo = self.sb("wmo", [128, 8, D], BF16)
    P.dma("pool", "wmo", w=["wmo"], out=wo[:], in_=w_o.rearrange("(k p) n -> p k n", p=128))
    for c in range(8):
        for T in range(4):
            cs = slice(T * 512, (T + 1) * 512)
            ip = 4 + (self.psrr % 4)
            self.psrr += 1
            po, pok = self.ps[ip], self.psk[ip]
            for kc in range(8):
                P.op("pe", "matmul", r=["wmo", f"h{kc}.{T}"], w=[pok], out=po[:], lhsT=wo[:, kc, c * 128:(c + 1) * 128], rhs=self.hT[kc][:, cs],
                     start=(kc == 0), stop=(kc == 7))
            P.op("dve", "tensor_tensor", r=[pok, f"x{c}.{T}"], w=[f"x{c}.{T}"], out=self.xT[c][:, cs], in0=po[:], in1=self.xT[c][:, cs], op=ALU.add)


KB.mem_attn = _mem_attn
KB.mix_out = _mix_out


def _conv_proj(self, w_in, uT, bgT, qxT):
    P = self.P
    cgs = self.ring("cgs", 2, [128, NT], F32)
    for ch in range(6):
        self.proj_fm_store(w_in[:, ch * 128:(ch + 1) * 128], 128, bgT[ch * 128:(ch + 1) * 128, :], f32=True)
    for ch in range(6):
        cg, cgk = cgs()

        def evac_c(pp, ppk, t, cg=cg, cgk=cgk):
            P.op("act", "activation", r=[ppk], w=[f"{cgk}.{t}"], out=cg[:, t * 512:(t + 1) * 512], in_=pp[:], func=AF.Copy)
        self.proj_fm(w_in[:, 768 + ch * 128:768 + (ch + 1) * 128], 128, evac_c)
        st, sk = self.stgf()

        def evac_v(pp, ppk, t, cg=cg, cgk=cgk, st=st, sk=sk):
            P.op("dve", "tensor_tensor", r=[ppk, f"{cgk}.{t}"], w=[f"{sk}.{t}"], out=st[:, t * 512:(t + 1) * 512], in0=pp[:],
                 in1=cg[:, t * 512:(t + 1) * 512], op=ALU.mult)
        self.proj_fm(w_in[:, 1536 + ch * 128:1536 + (ch + 1) * 128], 128, evac_v)
        P.dma("sp", sk, r=[f"{sk}.{t}" for t in range(NTILE)], w=[f"{sk}.{t}" for t in range(NTILE)] + ["dram.u"],
              out=uT[ch * 128:(ch + 1) * 128, :], in_=st[:])
    for ch in range(2):
        self.proj_fm_store(w_in[:, 2304 + ch * 128:2304 + (ch + 1) * 128], 128, qxT[ch * 128:(ch + 1) * 128, :], scale=0.125)


def _conv_mix(self, uT, halo, bgT, conv_w):
    P = self.P
    wc = self.sb("wc", [128, 3, 6], F32)
    for kk in range(3):
        P.dma("sp", "wc", w=[f"wc.{kk}"], out=wc[:, kk, :], in_=conv_w[kk].rearrange("(c p) -> p c", p=128), allow_slow_non_contiguous=True)
    U = self.ring("U", 2, [128, 16, 130], F32)
    BG = self.ring("BG", 2, [128, 16, 128], F32)
    AC = self.ring("AC", 2, [128, 16, 128], F32)
    for ch in range(6):
        u, uk = U()
        bg, bk = BG()
        ac, ak = AC()
        rows = slice(ch * 128, (ch + 1) * 128)
        P.dma("sp", uk, w=[uk + ".u"], out=u[:, :, 2:130], in_=uT[rows, :].rearrange("p (j q) -> p j q", q=128))
        P.dma("sp", uk, w=[uk + ".h"], out=u[:, :, 0:2], in_=halo[rows, :, :])
        P.dma("sp", bk, w=[bk], out=bg[:], in_=bgT[rows, :].rearrange("p (j q) -> p j q", q=128))
        uks = [uk + ".u", uk + ".h"]
        wck = ["wc.0", "wc.1", "wc.2"]
        P.op("dve", "tensor_scalar", r=uks + wck, w=[ak], out=ac[:], in0=u[:, :, 2:130], scalar1=wc[:, 2, ch:ch + 1], scalar2=None, op0=ALU.mult)
        P.op("dve", "scalar_tensor_tensor", r=uks + wck + [ak], w=[ak], out=ac[:], in0=u[:, :, 1:129], scalar=wc[:, 1, ch:ch + 1], in1=ac[:],
             op0=ALU.mult, op1=ALU.add)
        P.op("dve", "scalar_tensor_tensor", r=uks + wck + [ak], w=[ak], out=ac[:], in0=u[:, :, 0:128], scalar=wc[:, 0, ch:ch + 1], in1=ac[:],
             op0=ALU.mult, op1=ALU.add)
        P.op("pool", "tensor_tensor", r=[ak, bk], w=[f"h{ch}.{t}" for t in range(NTILE)], out=self.hT[ch][:].rearrange("p (j q) -> p j q", q=128),
             in0=ac[:], in1=bg[:], op=ALU.mult)


KB.conv_proj = _conv_proj
KB.conv_mix = _conv_mix


ROPE_THETA = 500000.0
TWO_PI = float(2 * np.pi)


def host_rope_cols():
    c = np.zeros((128, 2), np.float32)
    for p in range(128):
        d = p % 64
        if d < 16:
            c[p, 0] = ROPE_THETA ** (-(d % 8) / 8.0)
            c[p, 1] = -1.0 if d < 8 else 1.0
    return c


def host_swap_cols(w, ncols):
    idx = np.arange(ncols)
    d = idx % 64
    src = np.where(d < 8, idx + 8, np.where(d < 16, idx - 8, idx))
    return np.ascontiguousarray(w[:, src])


def _rope_tables(self, pos_d, ropec_d):
    P = self.P
    rc = self.sb("ropec", [128, 2], F32)
    P.dma("sp", "ropec", w=["ropec"], out=rc[:], in_=ropec_d)
    ang = self.sb("ang", [128, NT], F32)
    P.dma("sp", "ang", w=["ang"], out=ang[:], in_=pos_d.partition_broadcast(128))
    P.op("dve", "tensor_scalar", r=["ang", "ropec"], w=["ang"], out=ang[:], in0=ang[:], scalar1=rc[:, 0:1], scalar2=None, op0=ALU.mult)
    I32 = mybir.dt.int32
    C1 = 6.28125
    C2 = float(2 * np.pi - 6.28125)
    PI = float(np.pi)
    ki = self.sb("rki", [128, NT], I32)
    r = self.sb("rr", [128, NT], F32)
    m = self.sb("rm", [128, NT], F32)
    self.Ck = self.sb("Ck", [128, NT], F32)
    self.Sk = self.sb("Sk", [128, NT], F32)
    P.op("dve", "tensor_scalar", r=["ang"], w=["rki"], out=ki[:], in0=ang[:], scalar1=float(1.0 / (2 * np.pi)), scalar2=None, op0=ALU.mult)
    P.op("dve", "tensor_copy", r=["rki"], w=["rr"], out=r[:], in_=ki[:])
    P.op("dve", "scalar_tensor_tensor", r=["rr", "ang"], w=["rm"], out=m[:], in0=r[:], scalar=-C1, in1=ang[:], op0=ALU.mult, op1=ALU.add)
    P.op("dve", "scalar_tensor_tensor", r=["rr", "rm"], w=["rr"], out=r[:], in0=r[:], scalar=-C2, in1=m[:], op0=ALU.mult, op1=ALU.add)
    P.op("dve", "tensor_scalar", r=["rr"], w=["rm"], out=m[:], in0=r[:], scalar1=PI, scalar2=None, op0=ALU.is_gt)
    P.op("dve", "scalar_tensor_tensor", r=["rr", "rm"], w=["rm"], out=m[:], in0=m[:], scalar=-TWO_PI, in1=r[:], op0=ALU.mult, op1=ALU.add)
    P.op("dve", "tensor_scalar", r=["rm"], w=["rm"], out=m[:], in0=m[:], scalar1=-PI, scalar2=PI, op0=ALU.max, op1=ALU.min)
    P.op("act", "activation", r=["rm"], w=["rm"], out=m[:], in_=m[:], func=AF.Sin)
    P.op("dve", "tensor_scalar", r=["rm", "ropec"], w=["Sk"], out=self.Sk[:], in0=m[:], scalar1=rc[:, 1:2], scalar2=None, op0=ALU.mult)
    P.op("dve", "tensor_scalar", r=["rr", "Sk"], w=["rr"], out=r[:], in0=r[:], scalar1=float(np.pi / 2), scalar2=None, op0=ALU.add)
    P.op("dve", "tensor_scalar", r=["rr"], w=["rm"], out=m[:], in0=r[:], scalar1=PI, scalar2=None, op0=ALU.is_gt)
    P.op("dve", "scalar_tensor_tensor", r=["rr", "rm"], w=["rm"], out=m[:], in0=m[:], scalar=-TWO_PI, in1=r[:], op0=ALU.mult, op1=ALU.add)
    P.op("dve", "tensor_scalar", r=["rm"], w=["rm"], out=m[:], in0=m[:], scalar1=-PI, scalar2=PI, op0=ALU.max, op1=ALU.min)
    P.op("act", "activation", r=["rm"], w=["Ck"], out=self.Ck[:], in_=m[:], func=AF.Sin)


def _proj_rope_store(self, w_cols, wsw_cols, dst, scale):
    Ct, St, ck, sk_ = self.Ck, self.Sk, "Ck", "Sk"
    P = self.P
    st, sk = self.stg()
    tA = self.ropeA

    def evacA(pp, ppk, t):
        ta, tak = tA()
        self._last_ta = (ta, tak)
        P.op("dve", "tensor_tensor", r=[ppk, ck], w=[tak], out=ta[:], in0=pp[:], in1=Ct[:, t * 512:(t + 1) * 512], op=ALU.mult)
        self._tas[t] = (ta, tak)
    self._tas = {}
    self.proj_fm(w_cols, 128, evacA)

    def evacB(pp, ppk, t):
        ta, tak = self._tas[t]
        tb, tbk = self.ropeB()
        P.op("dve", "tensor_tensor", r=[ppk, sk_], w=[tbk], out=tb[:], in0=pp[:], in1=St[:, t * 512:(t + 1) * 512], op=ALU.mult)
        P.op("pool", "tensor_tensor", r=[tak, tbk], w=[tbk], out=tb[:], in0=ta[:], in1=tb[:], op=ALU.add)
        P.op("act", "activation", r=[tbk], w=[f"{sk}.{t}"], out=st[:, t * 512:(t + 1) * 512], in_=tb[:], func=AF.Copy, scale=scale)
    self.proj_fm(wsw_cols, 128, evacB)
    P.dma("sp", sk, r=[f"{sk}.{t}" for t in range(NTILE)], w=[f"{sk}.{t}" for t in range(NTILE)] + ["dram." + dst.tensor.name], out=dst, in_=st[:])


def _nsa_proj(self, w_in, w_sw, pos_d, ropec_d, qT, kcT, vcT, ksT, vs_tm, kwT, vw_tm, sgT, qxT):
    P = self.P
    self.rope_tables(pos_d, ropec_d)
    self.ropeA = self.ring("ropeA", 5, [128, 512], F32)
    self.ropeB = self.ring("ropeB", 3, [128, 512], F32)
    for ch in range(6):
        self.proj_rope_store(w_in[:, ch * 128:(ch + 1) * 128], w_sw[:, ch * 128:(ch + 1) * 128], qT[ch * 128:(ch + 1) * 128, :], 0.125)
    for i, (off, dst) in enumerate(((768, kcT), (1280, ksT), (1792, kwT))):
        for ch in range(2):
            self.proj_rope_store(w_in[:, off + ch * 128:off + (ch + 1) * 128], w_sw[:, 768 + i * 256 + ch * 128:768 + i * 256 + (ch + 1) * 128],
                                 dst[ch * 128:(ch + 1) * 128, :], 1.0)
    for ch in range(2):
        self.proj_fm_store(w_in[:, 1024 + ch * 128:1024 + (ch + 1) * 128], 128, vcT[ch * 128:(ch + 1) * 128, :])
    for ch in range(2):
        self.proj_fm_store(w_in[:, 2340 + ch * 128:2340 + (ch + 1) * 128], 128, qxT[ch * 128:(ch + 1) * 128, :], scale=0.125)
    wg32 = self.sb("wg32", [128, 8, 36], F32)
    P.dma("sp", "wg32", w=["wg32"], out=wg32[:], in_=w_in[:, 2304:2340].rearrange("(c p) n -> p c n", p=128))
    wgb = self.sb("wgb", [128, 8, 36], BF16)
    P.op("dve", "tensor_copy", r=["wg32"], w=["wgb"], out=wgb[:], in_=wg32[:])
    sg = self.sb("sgst", [36, NT], F32)
    for t in range(NTILE):
        cs = slice(t * 512, (t + 1) * 512)
        ip = 4 + (self.psrr % 3)
        self.psrr += 1
        pp, ppk = self.ps[ip], self.psk[ip]
        for c in range(8):
            P.op("pe", "matmul", r=["wgb", f"h{c}.{t}"], w=[ppk], out=pp[0:36, :], lhsT=wgb[:, c, :], rhs=self.hT[c][:, cs], start=(c == 0), stop=(c == 7))
        P.op("act", "activation", r=[ppk], w=[f"sg.{t}"], out=sg[:, cs], in_=pp[0:36, :], func=AF.Sigmoid)
    P.dma("sp", "sgst", r=[f"sg.{t}" for t in range(NTILE)], w=["dram.sg"], out=sgT, in_=sg[:])
    wv = self.sb("wvsw", [128, 8, 512], BF16)
    P.dma("pool", "wvsw", w=["wvsw.a"], out=wv[:, :, 0:256], in_=w_in[:, 1536:1792].rearrange("(c p) n -> p c n", p=128))
    P.dma("pool", "wvsw", w=["wvsw.b"], out=wv[:, :, 256:512], in_=w_in[:, 2048:2304].rearrange("(c p) n -> p c n", p=128))
    vst = self.ring("vst2", 3, [128, 512], BF16)
    for tb in range(16):
        t = tb // 4
        ts_ = slice(tb * 128, (tb + 1) * 128)
        vs, vk = vst()
        ip = 4 + (self.psrr % 3)
        self.psrr += 1
        pp, ppk = self.ps[ip], self.psk[ip]
        for c in range(8):
            P.op("pe", "matmul", r=["wvsw.a", "wvsw.b", f"h{c}.{t}"], w=[ppk], out=pp[:], lhsT=self.hT[c][:, ts_], rhs=wv[:, c, :], start=(c == 0), stop=(c == 7))
        P.op("act", "activation", r=[ppk], w=[vk], out=vs[:], in_=pp[:], func=AF.Copy)
        P.dma("sp", vk, r=[vk], w=[vk, "dram.vs"], out=vs_tm[ts_, :], in_=vs[:, 0:256])
        P.dma("sp", vk + "w", r=[vk], w=[vk, "dram.vw"], out=vw_tm[ts_, :], in_=vs[:, 256:512])


KB.rope_tables = _rope_tables
KB.proj_rope_store = _proj_rope_store
KB.nsa_proj = _nsa_proj


BIGF = 1.0e4


def host_ovA():
    o = np.zeros((128, 8, 257), np.float32)
    for m in range(8):
        for pn in range(128):
            n = 128 * m - 1 + pn
            if n < 0 or n > 1022:
                continue
            for j in range(256):
                if 16 * n < 64 * j + 64 and 16 * n + 32 > 64 * j:
                    o[pn, m, j] = 1.0
            o[pn, m, 256] = 1.0
    return o.reshape(128, 8 * 257)


def host_gsel():
    g = np.zeros((100, 36, 128), np.float32)
    for r in range(36):
        g[r, r, :] = 1.0
        g[64 + r, r, :] = 1.0
    return g.reshape(100, 36 * 128)


def host_epat():
    e = np.zeros((4, 64, 1024), np.float32)
    for jm in range(4):
        for i in range(16):
            e[jm, 16 * jm + i, 64 * i:64 * (i + 1)] = -NEG
    return e


def host_wmask():
    ps = np.arange(128)[:, None]
    pq = np.arange(128)[None, :]
    w = np.zeros((128, 2, 128), np.float32)
    w[:, 0, :] = np.where(ps > pq, 0.0, NEG)
    w[:, 1, :] = np.where(ps <= pq, 0.0, NEG)
    return w.reshape(128, 256)


def host_cmask(c):
    pn = np.arange(128)[:, None]
    pq = np.arange(128)[None, :]
    m = np.zeros((128, 2, 128), np.float32)
    m[:, 0, :] = np.where(16 * pn + 15 <= 128 * c + pq, 0.0, NEG)
    m[:, 1, :] = np.where(16 * pn - 1009 <= 128 * c + pq, 0.0, NEG)
    return m.reshape(128, 256)


def host_adj(c):
    a = np.zeros((16, 128, 256), np.float32)
    js = np.arange(256)[None, :]
    for j in range(16):
        tblk = (16 * j + 2 * c + (np.arange(128) >= 64).astype(np.int64))[:, None]
        forced = (js == 0) | (js == tblk) | (js == tblk - 1)
        valid = js <= tblk
        a[j] = np.where(valid, np.where(forced, BIGF, 0.0), -BIGF)
    return a


def host_kvwin(c, kwT_all, vw_all):
    dt = kwT_all.dtype
    kwin = np.zeros((16, 4, 65, 640), dt)
    vwin = np.zeros((16, 640, 256), dt)
    for j in range(16):
        gb = 8 * j + c
        for bi in range(5):
            b = gb - 4 + bi
            if b < 0:
                kwin[j, :, 64, bi * 128:(bi + 1) * 128] = NEG
                continue
            cc, jj = b % 8, b // 8
            kwin[j, :, 0:64, bi * 128:(bi + 1) * 128] = kwT_all[cc][:, jj * 128:(jj + 1) * 128].reshape(4, 64, 128)
            vwin[j, bi * 128:(bi + 1) * 128, :] = vw_all[cc][jj * 128:(jj + 1) * 128, :]
    return kwin, vwin


def _nsa_compress(self, kcT_all, vcT_all, cmp_pos, cmp_w1, cmp_w2):
    P = self.P
    self.kcmpT = self.sb("kcmpT", [64, 4, 1024], BF16)
    self.vcA = self.sb("vcA", [128, 4, 8, 128], BF16)
    P.op("pool", "memset", w=["vcA.one"], ap=self.vcA[:, :, :, 64:128], constant=1.0)
    P.op("pool", "memset", r=["vcA.one"], w=["vcA.one"], ap=self.vcA[0:1, :, 0, 64:128], constant=0.0)
    P.op("pool", "memset", w=["kcmpT.0"], ap=self.kcmpT[:, :, 0:1], constant=0.0)
    with self.scope():
        full = self.ring("cfull", 2, [64, S], BF16)
        h1r = self.ring("h1T", 2, [64, 1024], BF16)
        for kv, src in ((0, kcT_all), (1, vcT_all)):
            w1 = self.sb(f"cw1_{kv}", [64, 32, 64], BF16)
            P.dma("pool", f"cw1_{kv}", w=[f"cw1_{kv}"], out=w1[:], in_=cmp_w1[kv].rearrange("(l d) h -> d l h", d=64))
            w2 = self.sb(f"cw2_{kv}", [64, 64], BF16)
            P.dma("pool", f"cw2_{kv}", w=[f"cw2_{kv}"], out=w2[:], in_=cmp_w2[kv])
            pe32 = self.sb(f"cpe32_{kv}", [64, 32], F32)
            P.dma("sp", f"cpe_{kv}", w=[f"cpe32_{kv}"], out=pe32[:], in_=cmp_pos[kv].rearrange("l d -> d l"), allow_slow_non_contiguous=True)
            peb = self.sb(f"cpeb_{kv}", [64, 32], BF16)
            P.op("dve", "tensor_copy", r=[f"cpe32_{kv}"], w=[f"cpeb_{kv}"], out=peb[:], in_=pe32[:])
            b1 = self.sb(f"cb1_{kv}", [64, 1], F32)
            pb, pk = self.ps[7], self.psk[7]
            for l in range(32):
                P.op("pe", "matmul", r=[f"cw1_{kv}", f"cpeb_{kv}"], w=[pk], out=pb[0:64, 0:1], lhsT=w1[:, l, :], rhs=peb[:, l:l + 1], start=(l == 0), stop=(l == 31))
            P.op("dve", "tensor_copy", r=[pk], w=[f"cb1_{kv}"], out=b1[:], in_=pb[0:64, 0:1])
            for g in range(4):
                fu, fk = full()
                fv = fu[:].rearrange("d (j c p) -> d j c p", j=16, c=8, p=128)
                for cp in range(8):
                    P.dma("sp", fk, w=[f"{fk}.{cp}"], out=fv[:, :, cp, :], in_=src[cp, g * 64:(g + 1) * 64, :].rearrange("d (j p) -> d j p", p=128))
                fks = [f"{fk}.{cp}" for cp in range(8)]
                h1, hk = h1r()
                P.op("pool", "memset", w=[hk + ".z"], ap=h1[:, 0:1], constant=0.0)
                for (n0, ncols) in ((0, 512), (512, 511)):
                    ip = 4 + (self.psrr % 3)
                    self.psrr += 1
                    pp, ppk = self.ps[ip], self.psk[ip]
                    for l in range(32):
                        a = l + 16 * n0
                        P.op("pe", "matmul", r=fks + [f"cw1_{kv}"], w=[ppk], out=pp[0:64, 0:ncols], lhsT=w1[:, l, :],
                             rhs=fu[:, a:a + 16 * (ncols - 1) + 1:16], start=(l == 0), stop=(l == 31))
                    P.op("act", "activation", r=[ppk, f"cb1_{kv}"], w=[f"{hk}.{n0}"], out=h1[:, 1 + n0:1 + n0 + ncols], in_=pp[0:64, 0:ncols],
                         func=AF.Silu, bias=b1[:, 0:1])
                hks = [hk + ".z", f"{hk}.0", f"{hk}.512"]
                if kv == 0:
                    for half in range(2):
                        ip = 4 + (self.psrr % 3)
                        self.psrr += 1
                        pp, ppk = self.ps[ip], self.psk[ip]
                        P.op("pe", "matmul", r=hks + [f"cw2_{kv}"], w=[ppk], out=pp[0:64, :], lhsT=w2[:], rhs=h1[:, half * 512:(half + 1) * 512], start=True, stop=True)
                        lo = 1 if half == 0 else 0
                        P.op("act", "activation", r=[ppk], w=[f"kcmpT.{g}.{half}"], out=self.kcmpT[:, g, half * 512 + lo:(half + 1) * 512],
                             in_=pp[0:64, lo:512], func=AF.Copy)
                else:
                    for m in range(8):
                        ip = 4 + (self.psrr % 3)
                        self.psrr += 1
                        pp, ppk = self.ps[ip], self.psk[ip]
                        P.op("pe", "matmul", r=hks + [f"cw2_{kv}"], w=[ppk], out=pp[:, 0:64], lhsT=h1[:, m * 128:(m + 1) * 128], rhs=w2[:], start=True, stop=True)
                        P.op("act", "activation", r=[ppk], w=[f"vcA.{g}.{m}"], out=self.vcA[:, g, m, 0:64], in_=pp[:, 0:64], func=AF.Copy)


KB.nsa_compress = _nsa_compress


def _nsa_attn(self, qT, ksT_all, vs_all, kwin_d, vwin_d, sgT, adj_d, cmask_d, maskadd_d, ovA_d, gsel_d, epat_d, wmask_d):
    P = self.P
    ovA = self.sb("ovA", [128, 8, 257], BF16)
    P.dma("pool", "ovA", w=["ovA"], out=ovA[:], in_=ovA_d.rearrange("p (m j) -> p m j", m=8))
    gsel = self.sb("gsel", [100, 36, 128], BF16)
    P.dma("pool", "gsel", w=["gsel"], out=gsel[:], in_=gsel_d.rearrange("p (r q) -> p r q", r=36))
    madd = self.sb("madd", [128, 8, 128], BF16)
    P.dma("pool", "madd", w=["madd"], out=madd[:], in_=maskadd_d.rearrange("p (b q) -> p b q", b=8))
    cmask = self.sb("cmask", [128, 2, 128], BF16)
    P.dma("pool", "cmask", w=["cmask"], out=cmask[:], in_=cmask_d.rearrange("p (b q) -> p b q", b=2))
    wmask = self.sb("wmask", [128, 2, 128], BF16)
    P.dma("pool", "wmask", w=["wmask"], out=wmask[:], in_=wmask_d.rearrange("p (b q) -> p b q", b=2))
    sgh = self.sb("sgh", [100, NT], BF16)
    P.op("pool", "memset", w=["sgh.z"], ap=sgh[:], constant=0.0)
    with self.scope():
        sg32 = self.sb("sg32", [36, NT], F32)
        P.dma("sp", "sg32", w=["sg32"], out=sg32[:], in_=sgT)
        P.op("act", "activation", r=["sg32", "sgh.z"], w=["sgh.hi"], out=sgh[0:36, :], in_=sg32[:], func=AF.Copy)
        P.op("dve", "tensor_tensor", r=["sg32", "sgh.hi"], w=["sgh.lo"], out=sgh[64:100, :], in0=sg32[:], in1=sgh[0:36, :], op=ALU.subtract)
    sghk = ["sgh.z", "sgh.hi", "sgh.lo"]
    QA = self.sb("QA", [128, 4, 3, 512], BF16)
    QW = self.sb("QW", [65, 3, 512], BF16)
    P.op("pool", "memset", w=["QW.one"], ap=QW[64:65, :, :], constant=1.0)
    SELT = self.sb("SELT", [128, 4, 128], BF16)
    P.op("pool", "memset", w=["SELT.z"], ap=SELT[:], constant=0.0)
    KS = self.ring("KS", 3, [128, 1024], BF16)
    VS = self.ring("VS", 2, [128, 8, 128], BF16)
    for _ in range(2):
        vt, vk = VS()
        P.op("pool", "memset", w=[vk + ".one"], ap=vt[:, :, 64:128], constant=1.0)
    KW = self.ring("KW", 2, [65, 640], BF16)
    VW = self.ring("VW", 2, [128, 5, 128], BF16)
    for _ in range(2):
        vt, vk = VW()
        P.op("pool", "memset", w=[vk + ".one"], ap=vt[:, :, 64:128], constant=1.0)
    ysum = [self.sb(f"ysum{i}", [64, 512], F32) for i in range(3)]
    impacc = self.sb("impacc", [128, 4, 256], F32)
    adjr = self.ring("adj", 2, [128, 256], F32)
    scr = self.ring("scr", 2, [128, 256], F32)
    m8 = self.ring("m8", 2, [128, 16], F32)
    coefr = self.ring("coef", 2, [128, 512], F32)
    ctr = self.ring("ctr", 2, [64, 512], F32)
    rzq = self.ring("rzq", 4, [128, 1], F32)
    tps = self.ps[7][:].bitcast(BF16)
    gps, gpk = self.ps[6], self.psk[6]
    sti = [0]

    def st_ring(banks):
        si = banks[sti[0] % len(banks)]
        sti[0] += 1
        return self.ps[si], self.psk[si]

    def finish_branch(acc, acck, b, h, hi_, T, first):
        cs = slice(T * 512, (T + 1) * 512)
        rz, rzk = self.rz()
        P.op("dve", "tensor_scalar", r=[acck], w=[rzk], out=rz[64:128, :], in0=acc[64:128, :], scalar1=1e-30, scalar2=None, op0=ALU.max)
        P.op("dve", "reciprocal", r=[rzk], w=[rzk], out=rz[64:128, :], in_=rz[64:128, :])
        P.op("pe", "matmul", r=["gsel"] + sghk, w=[gpk], out=gps[:], lhsT=gsel[:, b * 12 + h, :], rhs=sgh[:, cs], start=True, stop=True)
        co, cok = coefr()
        P.op("dve", "tensor_tensor", r=[rzk, gpk], w=[cok], out=co[64:128, :], in0=rz[64:128, :], in1=gps[64:128, :], op=ALU.mult)
        if first:
            P.op("dve", "tensor_tensor", r=[acck, cok], w=[f"ysum{hi_}"], out=ysum[hi_][:], in0=acc[0:64, :], in1=co[64:128, :], op=ALU.mult)
        else:
            ct, ctk = ctr()
            P.op("dve", "tensor_tensor", r=[acck, cok], w=[ctk], out=ct[:], in0=acc[0:64, :], in1=co[64:128, :], op=ALU.mult)
            P.op("pool", "tensor_tensor", r=[ctk, f"ysum{hi_}"], w=[f"ysum{hi_}"], out=ysum[hi_][:], in0=ct[:], in1=ysum[hi_][:], op=ALU.add)

    for g in range(4):
        for T in range(4):
            cs = slice(T * 512, (T + 1) * 512)
            qsrc = qT[3 * g * 64:(3 * g + 3) * 64, cs].rearrange("(h d) n -> d h n", h=3)
            for kg in range(T + 1):
                P.dma("sp", f"QA{kg}", w=[f"QA.q{kg}"], out=QA[0:64, kg, :, :], in_=qsrc)
            P.dma("sp", "QW", w=["QW.q"], out=QW[0:64, :, :], in_=qsrc)
            nm = 2 * T + 2
            for hi_ in range(3):
                h = 3 * g + hi_
                acc, acck = self.ps[4], self.psk[4]
                for m in range(nm):
                    c0 = 256 if m == 2 * T + 1 else 0
                    ncol = 512 - c0
                    sp_, spk = st_ring([5, 7])
                    masked = (m >= 2 * T)
                    P.op("pe", "matmul", r=[f"kcmpT.{g}.0", f"kcmpT.{g}.1", "kcmpT.0", "QA.q0"], w=[spk], out=sp_[:, 0:ncol],
                         lhsT=self.kcmpT[:, g, m * 128:(m + 1) * 128], rhs=QA[0:64, 0, hi_, c0:512], start=True, stop=(not masked))
                    if masked:
                        P.op("pe", "matmul", r=["ident_bf", "cmask"], w=[spk], out=sp_[:, 0:128], lhsT=self.ident_bf[:], rhs=cmask[:, 0, :], start=False, stop=False)
                        P.op("pe", "matmul", r=["ident_bf", "cmask"], w=[spk], out=sp_[:, 128:256], lhsT=self.ident_bf[:], rhs=cmask[:, 1, :], start=False, stop=True)
                    pt, ptk = self.pt()
                    P.op("act", "activation", r=[spk], w=[ptk], out=pt[:, 0:ncol], in_=sp_[:, 0:ncol], func=AF.Exp)
                    P.op("pe", "matmul", r=[ptk, "vcA.one", f"vcA.{g}.{m}"], w=[acck], out=acc[:, c0:512], lhsT=self.vcA[:, g, m, :], rhs=pt[:, 0:ncol],
                         start=(m == 0), stop=(m == nm - 1))
                    for jj in range(c0 // 128, 4):
                        lastm = 2 * T if jj < 2 else 2 * T + 1
                        P.op("pe", "matmul", r=[ptk, "ovA"], w=[self.psk[jj]], out=self.ps[jj][:, 0:257], lhsT=pt[:, jj * 128 - c0:(jj + 1) * 128 - c0],
                             rhs=ovA[:, m, :], start=(m == 0), stop=(m == lastm))
                finish_branch(acc, acck, 0, h, hi_, T, True)
                for jj in range(4):
                    rq, rqk = rzq()
                    P.op("dve", "tensor_scalar", r=[self.psk[jj]], w=[rqk], out=rq[:], in0=self.ps[jj][:, 256:257], scalar1=1e-30, scalar2=None, op0=ALU.max)
                    P.op("dve", "reciprocal", r=[rqk], w=[rqk], out=rq[:], in_=rq[:])
                    if hi_ == 0:
                        P.op("dve", "tensor_scalar", r=[self.psk[jj], rqk], w=[f"impacc.{jj}"], out=impacc[:, jj, :], in0=self.ps[jj][:, 0:256], scalar1=rq[:, 0:1],
                             scalar2=None, op0=ALU.mult)
                    else:
                        P.op("dve", "scalar_tensor_tensor", r=[self.psk[jj], rqk, f"impacc.{jj}"], w=[f"impacc.{jj}"], out=impacc[:, jj, :], in0=self.ps[jj][:, 0:256],
                             scalar=rq[:, 0:1], in1=impacc[:, jj, :], op0=ALU.mult, op1=ALU.add)
            for jj in range(4):
                j = 4 * T + jj
                ad, adk = adjr()
                P.dma("sp", adk, w=[adk], out=ad[:], in_=adj_d[j])
                sc, sck = scr()
                P.op("dve", "tensor_tensor", r=[f"impacc.{jj}", adk], w=[sck], out=sc[:], in0=impacc[:, jj, :], in1=ad[:], op=ALU.add)
                mm, mk = m8()
                P.op("dve", "max", r=[sck], w=[mk + "a"], out=mm[:, 0:8], in_=sc[:])
                s2, s2k = scr()
                P.op("dve", "match_replace", r=[sck, mk + "a"], w=[s2k], out=s2[:], in_to_replace=mm[:, 0:8], in_values=sc[:], imm_value=-3.0e4)
                P.op("dve", "max", r=[s2k], w=[mk + "b"], out=mm[:, 8:16], in_=s2[:])
                P.op("dve", "tensor_scalar", r=[mk + "b"], w=[mk + "c"], out=mm[:, 15:16], in0=mm[:, 15:16], scalar1=-0.5 * BIGF, scalar2=None, op0=ALU.max)
                P.op("dve", "tensor_scalar", r=[sck, mk + "c", "SELT.z"], w=["SELT"], out=SELT[:, :, 64:128], in0=sc[:].rearrange("p (k r) -> p k r", k=4),
                     scalar1=mm[:, 15:16], scalar2=-1.0, op0=ALU.is_ge, op1=ALU.add)
                tp_key = self.psk[7]
                for kg in range(T + 1):
                    P.op("pe", "transpose", r=["SELT", "ident_bf"], w=[tp_key], out=tps[:, kg * 128:(kg + 1) * 128], in_=SELT[:, kg, :], identity=self.ident_bf[:])
                for hi_ in range(3):
                    eng = "act" if hi_ < 2 else "dve"
                    kw_ = dict(out=QA[64:128, 0:T + 1, hi_, jj * 128:(jj + 1) * 128], in_=tps[64:128, 0:(T + 1) * 128].rearrange("p (k q) -> p k q", k=T + 1))
                    if eng == "act":
                        P.op("act", "activation", r=[tp_key], w=[f"QA.s{hi_}"], func=AF.Copy, **kw_)
                    else:
                        P.op("dve", "tensor_copy", r=[tp_key], w=[f"QA.s{hi_}"], **kw_)
            for J in range(4 * T + 4):
                ks_, kk = KS()
                P.dma("sp", kk, w=[kk + ".k"], out=ks_[0:64, :].rearrange("d (c p) -> d c p", c=8),
                      in_=ksT_all[:, g * 64:(g + 1) * 64, J * 128:(J + 1) * 128].rearrange("c d p -> d c p"))
                P.dma("sp", kk, w=[kk + ".e"], out=ks_[64:128, :], in_=epat_d[J % 4])
                vs_, vk = VS()
                P.dma("sp", vk, w=[vk], out=vs_[:, :, 0:64], in_=vs_all[:, J * 128:(J + 1) * 128, g * 64:(g + 1) * 64].rearrange("c p d -> p c d"))
                jstart = max(J, 4 * T)
                c0 = (jstart - 4 * T) * 128
                ncol = 512 - c0
                diag = (J >= 4 * T)
                kg = J // 4
                for hi_ in range(3):
                    acc, acck = self.ps[hi_], self.psk[hi_]
                    for kb in range(8):
                        sp_, spk = st_ring([3, 4, 5, 7])
                        P.op("pe", "matmul", r=[kk + ".k", kk + ".e", f"QA.q{kg}", f"QA.s{hi_}"], w=[spk], out=sp_[:, 0:ncol], lhsT=ks_[:, kb * 128:(kb + 1) * 128],
                             rhs=QA[:, kg, hi_, c0:512], start=True, stop=(not diag))
                        if diag:
                            P.op("pe", "matmul", r=["ident_bf", "madd"], w=[spk], out=sp_[:, 0:128], lhsT=self.ident_bf[:], rhs=madd[:, kb, :], start=False, stop=True)
                        pt, ptk = self.pt()
                        P.op("act", "activation", r=[spk], w=[ptk], out=pt[:, 0:ncol], in_=sp_[:, 0:ncol], func=AF.Exp)
                        P.op("pe", "matmul", r=[ptk, vk, vk + ".one"], w=[acck], out=acc[:, c0:512], lhsT=vs_[:, kb, :], rhs=pt[:, 0:ncol],
                             start=(J == 0 and kb == 0), stop=(J == 4 * T + 3 and kb == 7))
            for hi_ in range(3):
                finish_branch(self.ps[hi_], self.psk[hi_], 1, 3 * g + hi_, hi_, T, False)
            for jj in range(4):
                j = 4 * T + jj
                kw_t, kwk = KW()
                P.dma("sp", kwk, w=[kwk], out=kw_t[:], in_=kwin_d[j, g])
                vw_t, vwk = VW()
                P.dma("sp", vwk, w=[vwk], out=vw_t[:, :, 0:64], in_=vwin_d[j, :, g * 64:(g + 1) * 64].rearrange("(b p) d -> p b d", p=128))
                for kb in range(5):
                    sp_, spk = st_ring([3, 4, 5, 7])
                    wm = kb in (0, 4)
                    P.op("pe", "matmul", r=[kwk, "QW.q", "QW.one"], w=[spk], out=sp_[:, 0:384].rearrange("p (h q) -> p h q", h=3),
                         lhsT=kw_t[:, kb * 128:(kb + 1) * 128], rhs=QW[:, :, jj * 128:(jj + 1) * 128], start=True, stop=(not wm))
                    if wm:
                        for hi_ in range(3):
                            P.op("pe", "matmul", r=["ident_bf", "wmask"], w=[spk], out=sp_[:, hi_ * 128:(hi_ + 1) * 128], lhsT=self.ident_bf[:],
                                 rhs=wmask[:, 0 if kb == 0 else 1, :], start=False, stop=(hi_ == 2))
                    pt, ptk = self.pt()
                    P.op("act", "activation", r=[spk], w=[ptk], out=pt[:, 0:384], in_=sp_[:, 0:384], func=AF.Exp)
                    for hi_ in range(3):
                        P.op("pe", "matmul", r=[ptk, vwk, vwk + ".one"], w=[self.psk[hi_]], out=self.ps[hi_][:, jj * 128:(jj + 1) * 128], lhsT=vw_t[:, kb, :],
                             rhs=pt[:, hi_ * 128:(hi_ + 1) * 128], start=(kb == 0), stop=(kb == 4))
            for hi_ in range(3):
                h = 3 * g + hi_
                finish_branch(self.ps[hi_], self.psk[hi_], 2, h, hi_, T, False)
                P.op("act", "activation", r=[f"ysum{hi_}"], w=[f"h{h // 2}.{T}"], out=self.hT[h // 2][(h % 2) * 64:(h % 2) * 64 + 64, cs], in_=ysum[hi_][:], func=AF.Copy)


KB.nsa_attn = _nsa_attn


import ml_dtypes
from concourse.bass_utils import run_bass_kernel_spmd

_PROJ_OUT = {
    "fox": [("qT", [768, NT], BF16), ("kT", [768, NT], BF16), ("v", [NT, 768], BF16), ("nlf", [NT, 12], F32), ("qxT", [256, NT], BF16)],
    "conv": [("uT", [768, NT], F32), ("bgT", [768, NT], F32), ("qxT", [256, NT], BF16)],
    "nsa": [("qT", [768, NT], BF16), ("kcT", [256, NT], BF16), ("vcT", [256, NT], BF16), ("ksT", [256, NT], BF16), ("vs", [NT, 256], BF16),
            ("kwT", [256, NT], BF16), ("vw", [NT, 256], BF16), ("sgT", [36, NT], F32), ("qxT", [256, NT], BF16)],
}


def _ffn_block(k, tag):
    g = k.dram_in(f"g_{tag}", [D]); wi = k.dram_in(f"wi_{tag}", [D, 2 * DFF]); wo = k.dram_in(f"wo_{tag}", [DFF, D])
    with k.scope():
        k.ffn_setup()
        k.rmsnorm(g)
        k.ffn(wi, wo)


def _proj_block(k, kind):
    gm = k.dram_in("gm", [D])
    outs = {n: k.dram_out("o_" + n, shp, dt) for (n, shp, dt) in _PROJ_OUT[kind]}
    with k.scope():
        k.proj_setup()
        k.rmsnorm(gm)
        if kind == "fox":
            w = k.dram_in("wproj", [D, 2572]); bf = k.dram_in("bf", [12])
            k.fox_proj(w, bf, outs["qT"], outs["kT"], outs["v"], outs["nlf"], outs["qxT"])
        elif kind == "conv":
            w = k.dram_in("wproj", [D, 2560])
            k.conv_proj(w, outs["uT"], outs["bgT"], outs["qxT"])
        else:
            w = k.dram_in("wproj", [D, 2596]); wsw = k.dram_in("wsw", [D, 1536]); pos = k.dram_in("pos", [NT]); ropec = k.dram_in("ropec", [128, 2])
            k.nsa_proj(w, wsw, pos, ropec, outs["qT"], outs["kcT"], outs["vcT"], outs["ksT"], outs["vs"], outs["kwT"], outs["vw"], outs["sgT"], outs["qxT"])


def _mix_block(k, kind):
    qxT = k.dram_in("qxT", [256, NT], BF16)
    memT = k.dram_in("memT", [8, 128, 256]); gmem = k.dram_in("gmem", [D]); wkv = k.dram_in("wkv", [D, 512]); wmo = k.dram_in("wmo", [D, D])
    if kind == "fox":
        qT = k.dram_in("qT", [768, NT], BF16); kT = k.dram_in("kT_all", [8, 768, NT], BF16); v = k.dram_in("v_all", [8, NT, 768], BF16)
        nlf = k.dram_in("nlf_all", [8, NT, 12], F32); madd = k.dram_in("maskadd", [128, 1024]); selm = k.dram_in("selmask", [128, 1536])
        with k.scope():
            k.attn_consts()
            k.fox_attn(qT, kT, v, nlf, madd, selm)
    elif kind == "conv":
        uT = k.dram_in("uT", [768, NT]); bgT = k.dram_in("bgT", [768, NT]); halo = k.dram_in("halo", [768, 16, 2]); cw = k.dram_in("cw", [3, 768])
        with k.scope():
            k.conv_mix(uT, halo, bgT, cw)
    else:
        qT = k.dram_in("qT", [768, NT], BF16)
        kcT = k.dram_in("kcT_all", [8, 256, NT], BF16); vcT = k.dram_in("vcT_all", [8, 256, NT], BF16)
        ksT = k.dram_in("ksT_all", [8, 256, NT], BF16); vs = k.dram_in("vs_all", [8, NT, 256], BF16)
        kwin = k.dram_in("kwin", [16, 4, 65, 640], BF16); vwin = k.dram_in("vwin", [16, 640, 256], BF16)
        sgT = k.dram_in("sgT", [36, NT], F32)
        adj = k.dram_in("adj", [16, 128, 256]); cmask = k.dram_in("cmask", [128, 256]); madd = k.dram_in("maskadd", [128, 1024])
        ovA = k.dram_in("ovA", [128, 8 * 257]); gsel = k.dram_in("gsel", [100, 36 * 128]); epat = k.dram_in("epat", [4, 64, 1024], BF16)
        wmask = k.dram_in("wmask", [128, 256])
        cpos = k.dram_in("cpos", [2, 32, 64]); cw1 = k.dram_in("cw1", [2, 2048, 64]); cw2 = k.dram_in("cw2", [2, 64, 64])
        with k.scope():
            k.attn_consts()
            k.nsa_compress(kcT, vcT, cpos, cw1, cw2)
            k.nsa_attn(qT, ksT, vs, kwin, vwin, sgT, adj, cmask, madd, ovA, gsel, epat, wmask)
    with k.scope():
        k.norm_setup()
        k.attn_consts()
        k.mem_attn(memT, gmem, wkv, qxT)
    with k.scope():
        k.mix_out(wmo)


def build_launch(li, kinds):
    k = KB()
    x_in = k.dram_in("xT_in", [8, 128, NT])
    x_out = k.dram_out("xT_out", [8, 128, NT])
    k.setup_common()
    k.load_xT(x_in)
    if li > 0:
        _mix_block(k, kinds[li - 1])
        _ffn_block(k, "b")
    if li < 4:
        _ffn_block(k, "a")
        k.store_xT(x_out)
        _proj_block(k, kinds[li])
    else:
        gf = k.dram_in("gfinal", [D])
        with k.scope():
            k.norm_setup()
            k.rmsnorm(gf, final=True)
        k.store_xT(x_out)
    return k.finish()


_KINDS = ["fox", "conv", "nsa", "fox"]


def kernel(x, mem, ffn1_norm, ffn1_w_in, ffn1_w_out, mix_norm, mix_w_out, mem_norm, mem_w_kv,
           fox_w_in, fox_b_f, conv_w_in, conv_w, nsa_w_in, nsa_cmp_pos, nsa_cmp_w1, nsa_cmp_w2,
           ffn2_norm, ffn2_w_in, ffn2_w_out, final_norm):
    f32 = lambda a: np.ascontiguousarray(np.asarray(a, dtype=np.float32))
    x = f32(x)[0]
    memT = np.ascontiguousarray(f32(mem)[0].T.reshape(8, 128, 256))
    cores = list(range(8))
    xs = [np.ascontiguousarray(x[local_tokens(c)].T.reshape(8, 128, NT)) for c in cores]
    proj = None
    for li in range(5):
        nc = build_launch(li, _KINDS)
        maps = []
        for c in cores:
            m = {"xT_in": xs[c]}
            if li > 0:
                i = li - 1
                kind = _KINDS[i]
                m.update({"qxT": proj[c]["o_qxT"], "memT": memT, "gmem": f32(mem_norm), "wkv": f32(mem_w_kv[i]), "wmo": f32(mix_w_out[i]),
                          "g_b": f32(ffn2_norm[i]), "wi_b": f32(ffn2_w_in[i]), "wo_b": f32(ffn2_w_out[i])})
                if kind == "fox":
                    m.update({"qT": proj[c]["o_qT"], "kT_all": st["kT"], "v_all": st["v"], "nlf_all": st["nlf"],
                              "maskadd": host_maskadd(c).reshape(128, 1024), "selmask": host_selmask(c)})
                elif kind == "conv":
                    m.update({"uT": proj[c]["o_uT"], "bgT": proj[c]["o_bgT"], "halo": st["halo"][c], "cw": f32(conv_w[0])})
                else:
                    m.update({"qT": proj[c]["o_qT"], "kcT_all": st["kcT"], "vcT_all": st["vcT"], "ksT_all": st["ksT"], "vs_all": st["vs"],
                              "kwin": st["kvwin"][c][0], "vwin": st["kvwin"][c][1], "sgT": proj[c]["o_sgT"], "adj": host_adj(c), "cmask": host_cmask(c),
                              "maskadd": host_maskadd(c).reshape(128, 1024), "ovA": st["ovA"], "gsel": st["gsel"], "epat": st["epat"], "wmask": st["wmask"],
                              "cpos": f32(nsa_cmp_pos[0]), "cw1": f32(nsa_cmp_w1[0]), "cw2": f32(nsa_cmp_w2[0])})
            if li < 4:
                kind = _KINDS[li]
                m.update({"g_a": f32(ffn1_norm[li]), "wi_a": f32(ffn1_w_in[li]), "wo_a": f32(ffn1_w_out[li]), "gm": f32(mix_norm[li])})
                if kind == "fox":
                    jj = li // 3
                    m.update({"wproj": f32(fox_w_in[jj]), "bf": f32(fox_b_f[jj])})
                elif kind == "conv":
                    m.update({"wproj": f32(conv_w_in[0])})
                else:
                    W = f32(nsa_w_in[0])
                    wsw = np.concatenate([host_swap_cols(W[:, 0:768], 768), host_swap_cols(W[:, 768:1024], 256),
                                          host_swap_cols(W[:, 1280:1536], 256), host_swap_cols(W[:, 1792:2048], 256)], 1)
                    m.update({"wproj": W, "wsw": np.ascontiguousarray(wsw), "pos": local_tokens(c).astype(np.float32), "ropec": host_rope_cols()})
            else:
                m.update({"gfinal": f32(final_norm)})
            maps.append(m)
        res = run_bass_kernel_spmd(nc, maps, core_ids=cores).results
        xs = [res[c]["xT_out"] for c in cores]
        if li < 4:
            proj = res
            kind = _KINDS[li]
            stack = lambda n: np.stack([res[c]["o_" + n] for c in cores])
            st = {}
            if kind == "fox":
                st = {"kT": stack("kT"), "v": stack("v"), "nlf": stack("nlf")}
            elif kind == "conv":
                uall = [res[c]["o_uT"] for c in cores]
                halos = []
                for c in cores:
                    h = np.zeros((768, 16, 2), np.float32)
                    for j in range(16):
                        gb = 8 * j + c - 1
                        if gb >= 0:
                            h[:, j, :] = uall[gb % 8][:, (gb // 8) * 128 + 126:(gb // 8) * 128 + 128]
                    halos.append(h)
                st = {"halo": halos}
            else:
                kw_all, vw_all = stack("kwT"), stack("vw")
                st = {"kcT": stack("kcT"), "vcT": stack("vcT"), "ksT": stack("ksT"), "vs": stack("vs"),
                      "kvwin": [host_kvwin(c, kw_all, vw_all) for c in cores], "ovA": host_ovA(), "gsel": host_gsel(),
                      "epat": host_epat().astype(ml_dtypes.bfloat16), "wmask": host_wmask()}
    out = np.zeros((S, D), np.float32)
    for c in cores:
        out[local_tokens(c)] = xs[c].reshape(D, NT).T
    return out[None]
```

```python
import numpy as np
import concourse.bass as bass
import concourse.mybir as mybir

F32 = mybir.dt.float32
BF16 = mybir.dt.bfloat16
AF = mybir.ActivationFunctionType
ALU = mybir.AluOpType
AX = mybir.AxisListType


class Prog:
    COMPUTE = ("pe", "act", "dve", "pool")

    def __init__(self, nc):
        self.nc = nc
        self.ops = []

    def op(self, eng, meth, r=(), w=(), **kw):
        self.ops.append(dict(eng=eng, fn=(meth, kw), r=tuple(r), w=tuple(w), dma=False, grp=None))
        return len(self.ops) - 1

    def dma(self, eng, grp, r=(), w=(), **kw):
        if grp is None:
            grp = tuple(w)[0]
        self.ops.append(dict(eng=eng, fn=("dma_start", kw), r=tuple(r), w=tuple(w), dma=True, grp=grp))
        return len(self.ops) - 1

    def dmaop(self, eng, meth, grp, r=(), w=(), inc=1, after_all_dma=False, **kw):
        self.ops.append(dict(eng=eng, fn=(meth, kw), r=tuple(r), w=tuple(w), dma=True, grp=grp, inc=inc, after_all=after_all_dma))
        return len(self.ops) - 1

    def finalize(self, final_keys=()):
        nc = self.nc
        ops = self.ops
        if not hasattr(self, "sems"):
            self.sems = {}
            self.semctx = []
            self.cnt = {}
            self.seen = {e: {} for e in ("pe", "act", "dve", "pool", "sp")}
            self.tot_ops = 0
        ops.append(dict(eng="sp", fn=None, r=tuple(final_keys), w=(), dma=False, grp=None, final=True))
        last_w = {}
        last_r = {}
        last_touch = {}
        deps = []
        for i, o in enumerate(ops):
            d = {}
            for k in set(o["r"]) | set(o["w"]):
                if k.startswith("ps"):
                    lt = last_touch.setdefault(k, {})
                    for e2, j in lt.items():
                        if e2 != o["eng"]:
                            d[j] = "excl"
                    lt[o["eng"]] = i
            for k in o["r"]:
                if k in last_w:
                    d[last_w[k]] = "raw"
            for k in o["w"]:
                if k in last_w:
                    d.setdefault(last_w[k], "waw")
                for rr in last_r.get(k, ()):
                    d.setdefault(rr, "war")
            if o.get("final"):
                for j, p in enumerate(ops):
                    if p["dma"]:
                        d.setdefault(j, "raw")
            if o.get("after_all"):
                for j in range(i):
                    if ops[j]["dma"]:
                        d.setdefault(j, "raw")
            d.pop(i, None)
            dd = []
            for j, kind in d.items():
                p = ops[j]
                if (not p["dma"]) and (not o["dma"]) and p["eng"] == o["eng"]:
                    if o["eng"] == "pe" or o["eng"] == "sp":
                        continue
                    if kind != "raw":
                        continue
                dd.append(j)
            deps.append(dd)
            for k in o["w"]:
                last_w[k] = i
                last_r[k] = []
            for k in o["r"]:
                last_r.setdefault(k, []).append(i)
        signaled = set()
        for dd in deps:
            signaled.update(dd)

        if not hasattr(self, "dmapool"):
            self.dmapool = []
        grp2phys = {}

        def getsem(name):
            if name not in self.sems:
                cm = nc.semaphore("s%d" % len(self.sems))
                sh = cm.__enter__()
                self.semctx.append(cm)
                self.sems[name] = sh
            return self.sems[name]

        def physname(grp):
            if grp not in grp2phys:
                idx = len(grp2phys)
                grp2phys[grp] = ("dmaphys", idx)
            return grp2phys[grp]

        cnt = self.cnt
        event = {}
        for i, o in enumerate(ops):
            if o["dma"]:
                sname = physname(o["grp"])
                cnt[sname] = cnt.get(sname, 0) + o.get("inc", 16)
                event[i] = (sname, cnt[sname])
            elif i in signaled:
                sname = ("eng", o["eng"])
                cnt[sname] = cnt.get(sname, 0) + 1
                event[i] = (sname, cnt[sname])
        streams = {e: [] for e in ("pe", "act", "dve", "pool", "sp")}
        seen = self.seen
        nwait = 0
        for i, o in enumerate(ops):
            e = o["eng"]
            need = {}
            for j in deps[i]:
                sname, val = event[j]
                need[sname] = max(need.get(sname, 0), val)
            for sname, val in need.items():
                if seen[e].get(sname, 0) >= val:
                    continue
                seen[e][sname] = val
                streams[e].append(("wait", getsem(sname), val))
                nwait += 1
            if o["fn"] is not None:
                ev = event.get(i)
                if ev is not None:
                    streams[e].append(("op", o["fn"], getsem(ev[0]), o.get("inc", 16) if o["dma"] else 1))
                else:
                    streams[e].append(("op", o["fn"], None, 0))
        self.streams = streams
        self.tot_ops += len(ops)
        self.stats = dict(nops=len(ops), tot=self.tot_ops, nwait=nwait, nsem=len(self.sems),
                          per_eng={e: len(st) for e, st in streams.items()},
                          maxcnt=max(cnt.values()) if cnt else 0)
        return streams

    def flush(self, final_keys=()):
        self.finalize(final_keys)
        self.emit()
        self.ops = []

    def close(self):
        for cm in reversed(getattr(self, "semctx", [])):
            cm.__exit__(None, None, None)
        self.semctx = []

    def emit(self):
        nc = self.nc
        streams = self.streams

        def run(engh, items):
            for it in items:
                if it[0] == "wait":
                    engh.wait_ge(it[1], it[2])
                else:
                    ins = getattr(engh, it[1][0])(**it[1][1])
                    if it[2] is not None:
                        ins.then_inc(it[2], it[3])

        with nc.Block() as block:
            @block.sync
            def _(e):
                run(e, streams["sp"])

            @block.tensor
            def _(e):
                run(e, streams["pe"])

            @block.scalar
            def _(e):
                run(e, streams["act"])

            @block.vector
            def _(e):
                run(e, streams["dve"])

            @block.gpsimd
            def _(e):
                run(e, streams["pool"])


import numpy as np
from contextlib import ExitStack
import concourse.bass as bass
import concourse.mybir as mybir

D = 1024
NT = 2048
NTILE = 4
S = 16384
DFF = 2816
NFB = 22
EPS = 1e-6


NEG = -30000.0


def local_tokens(c):
    return np.concatenate([np.arange((8 * j + c) * 128, (8 * j + c + 1) * 128) for j in range(16)])


def host_maskadd(c):
    m = np.zeros((128, 8, 128), np.float32)
    k = np.arange(128)[:, None]
    q = np.arange(128)[None, :]
    for kb in range(8):
        if kb == c:
            m[:, kb, :] = np.where(q >= k, 0.0, NEG)
        elif kb > c:
            m[:, kb, :] = NEG
    return m


def host_selmask(c):
    m = np.zeros((128, 16, 8, 12), np.float32)
    m[:, :, c, :] = 1.0
    return m.reshape(128, 1536)


class KB:
    def __init__(self):
        self.nc = bass.Bass("TRN2", target_bir_lowering=False, num_devices=8)
        self.P = Prog(self.nc)
        self.es = ExitStack()
        self.cur = self.es
        self.outs = []
        self.psrr = 0

    def dram_in(self, name, shape, dt=F32):
        return self.nc.dram_tensor(name, list(shape), dt, kind="ExternalInput").ap()

    def dram_out(self, name, shape, dt=F32):
        self.outs.append(name)
        return self.nc.dram_tensor(name, list(shape), dt, kind="ExternalOutput").ap()

    def sb(self, name, shape, dt):
        self.nsb = getattr(self, "nsb", 0) + 1
        return self.cur.enter_context(self.nc.sbuf_tensor(f"{name}_{self.nsb}", list(shape), dt))

    def scope(self):
        kb = self

        class _S:
            def __enter__(s_):
                s_.prev = kb.cur
                s_.st = ExitStack()
                kb.cur = s_.st
                return s_

            def __exit__(s_, *a):
                if a[0] is None:
                    kb.P.flush()
                    print("  flush", kb.P.stats, flush=True)
                s_.st.close()
                kb.cur = s_.prev
                return False
        return _S()

    def psum(self, name, shape, dt=F32):
        return self.es.enter_context(self.nc.psum_tensor(name, list(shape), dt))

    def ring(self, name, n, shape, dt):
        tiles = [self.sb(f"{name}{i}", shape, dt) for i in range(n)]
        st = {"i": 0}

        def nxt():
            i = st["i"] % n
            st["i"] += 1
            return tiles[i], f"{name}{i}"
        return nxt

    def setup_common(self):
        P = self.P
        self.ps = [self.psum(f"ps{i}", [128, 512], F32) for i in range(8)]
        self.psk = [f"ps{i}" for i in range(8)]
        self.ones_bf = self.sb("ones_bf", [128, 128], BF16)
        P.op("pool", "memset", w=["ones_bf"], ap=self.ones_bf[:], constant=1.0)
        self.eps_col = self.sb("eps_col", [128, 1], F32)
        P.op("pool", "memset", w=["eps"], ap=self.eps_col[:], constant=EPS)
        self.xT = [self.sb(f"xT{c}", [128, NT], F32) for c in range(8)]
        self.hT = [self.sb(f"hT{c}", [128, NT], BF16) for c in range(8)]

    def norm_setup(self):
        self.sq = self.ring("sq", 3, [128, 512], BF16)
        self.rstd = self.ring("rstd", 2, [128, 512], F32)
        self.gcol = self.ring("gcol", 2, [128, 8], F32)

    def load_xT(self, x_dram):
        for c in range(8):
            self.P.dma("sp", f"xT{c}", w=[f"x{c}.{t}" for t in range(NTILE)], out=self.xT[c][:], in_=x_dram[c])

    def store_xT(self, x_out):
        for c in range(8):
            self.P.dma("sp", f"xT{c}", r=[f"x{c}.{t}" for t in range(NTILE)], w=[f"xo{c}"], out=x_out[c], in_=self.xT[c][:])
        return [f"xo{c}" for c in range(8)]

    def rmsnorm(self, g_dram, ntile=NTILE, xT=None, hT=None, xk="x", hk="h", width=512, final=False):
        P = self.P
        xT = xT or self.xT
        hT = hT or self.hT
        if final:
            hT, hk = xT, xk
        gt, gk = self.gcol()
        P.dma("sp", gk, w=[gk], out=gt[:], in_=g_dram)
        for t in range(ntile):
            cs = slice(t * width, (t + 1) * width)
            pb, pk = self.ps[7], self.psk[7]
            for c in range(8):
                sq, sqk = self.sq()
                P.op("act", "activation", r=[f"{xk}{c}.{t}"], w=[sqk], out=sq[:, 0:width], in_=xT[c][:, cs], func=AF.Square)
                P.op("pe", "matmul", r=[sqk, "ones_bf"], w=[pk], out=pb[:, 0:width], lhsT=self.ones_bf[:], rhs=sq[:, 0:width],
                     start=(c == 0), stop=(c == 7))
            rs, rk = self.rstd()
            P.op("act", "activation", r=[pk, "eps"], w=[rk], out=rs[:, 0:width], in_=pb[:, 0:width], func=AF.Sqrt,
                 scale=1.0 / D, bias=self.eps_col[:, 0:1])
            P.op("dve", "reciprocal", r=[rk], w=[rk], out=rs[:, 0:width], in_=rs[:, 0:width])
            for c in range(8):
                P.op("dve", "scalar_tensor_tensor", r=[f"{xk}{c}.{t}", gk, rk], w=[f"{hk}{c}.{t}"],
                     out=hT[c][:, cs], in0=xT[c][:, cs], scalar=gt[:, c:c + 1], in1=rs[:, 0:width], op0=ALU.mult, op1=ALU.mult)

    def ffn_setup(self, nb=2):
        self.norm_setup()
        self.ffn_nb = nb
        self.wgu = self.ring("wgu", 3, [128, 8, 256], BF16)
        self.wo = self.ring("wo", 2, [128, nb, D], BF16)
        self.aT = [self.sb(f"aT{j}", [128, NT], BF16) for j in range(nb)]
        self.sg = self.ring("sg", 3, [128, 512], F32)

    def ffn(self, w_in, w_out):
        P = self.P
        nb = self.ffn_nb
        win_v = w_in.rearrange("(c p) n -> p c n", p=128)
        for part in range(NFB // nb):
            for jj in range(nb):
                j = part * nb + jj
                wt, wk = self.wgu()
                P.dma("pool", wk + "g", w=[wk + "g"], out=wt[:, :, 0:128], in_=win_v[:, :, j * 128:(j + 1) * 128])
                P.dma("pool", wk + "u", w=[wk + "u"], out=wt[:, :, 128:256], in_=win_v[:, :, DFF + j * 128:DFF + (j + 1) * 128])
                for t in range(NTILE):
                    cs = slice(t * 512, (t + 1) * 512)
                    ig = (self.psrr % 2) * 2
                    self.psrr += 1
                    pg, pgk = self.ps[ig], self.psk[ig]
                    pu, puk = self.ps[ig + 1], self.psk[ig + 1]
                    for c in range(8):
                        P.op("pe", "matmul", r=[wk + "g", f"h{c}.{t}"], w=[pgk], out=pg[:], lhsT=wt[:, c, 0:128], rhs=self.hT[c][:, cs],
                             start=(c == 0), stop=(c == 7))
                    for c in range(8):
                        P.op("pe", "matmul", r=[wk + "u", f"h{c}.{t}"], w=[puk], out=pu[:], lhsT=wt[:, c, 128:256], rhs=self.hT[c][:, cs],
                             start=(c == 0), stop=(c == 7))
                    sg, sgk = self.sg()
                    P.op("act", "activation", r=[pgk], w=[sgk], out=sg[:], in_=pg[:], func=AF.Silu)
                    P.op("dve", "tensor_tensor", r=[sgk, puk], w=[f"a{jj}.{t}"], out=self.aT[jj][:, cs], in0=sg[:], in1=pu[:], op=ALU.mult)
            wo, wok = self.wo()
            P.dma("pool", wok, w=[wok], out=wo[:],
                  in_=w_out[part * nb * 128:(part + 1) * nb * 128, :].rearrange("(j p) n -> p j n", p=128))
            for c in range(8):
                for t in range(NTILE):
                    cs = slice(t * 512, (t + 1) * 512)
                    ip = 4 + (self.psrr % 2)
                    self.psrr += 1
                    po, pok = self.ps[ip], self.psk[ip]
                    for jj in range(nb):
                        P.op("pe", "matmul", r=[wok, f"a{jj}.{t}"], w=[pok], out=po[:], lhsT=wo[:, jj, c * 128:(c + 1) * 128],
                             rhs=self.aT[jj][:, cs], start=(jj == 0), stop=(jj == nb - 1))
                    P.op("dve", "scalar_tensor_tensor", r=[pok, f"x{c}.{t}"], w=[f"x{c}.{t}"], out=self.xT[c][:, cs], in0=po[:],
                         scalar=0.5, in1=self.xT[c][:, cs], op0=ALU.mult, op1=ALU.add)

    def finish(self, final_keys=()):
        if self.P.ops:
            self.P.flush(final_keys)
            print("  flush", self.P.stats, flush=True)
        self.P.close()
        self.es.close()
        return self.nc


def _proj_setup(self):
    self.norm_setup()
    self.wp = self.ring("wp", 3, [128, 8, 128], BF16)
    self.stg = self.ring("stg", 3, [128, NT], BF16)
    self._stgf = None
    self.one_col = self.sb("one_col", [128, 1], F32)
    self.P.op("pool", "memset", w=["one_col"], ap=self.one_col[:], constant=1.0)


def _proj_fm(self, w_cols, n, evac):
    P = self.P
    wt, wk = self.wp()
    P.dma("pool", wk, w=[wk], out=wt[:, :, 0:n], in_=w_cols.rearrange("(c p) n -> p c n", p=128))
    for t in range(NTILE):
        cs = slice(t * 512, (t + 1) * 512)
        ip = 4 + (self.psrr % 3)
        self.psrr += 1
        pp, ppk = self.ps[ip], self.psk[ip]
        for c in range(8):
            P.op("pe", "matmul", r=[wk, f"h{c}.{t}"], w=[ppk], out=pp[0:n, :], lhsT=wt[:, c, 0:n], rhs=self.hT[c][:, cs],
                 start=(c == 0), stop=(c == 7))
        evac(pp, ppk, t)


def _proj_fm_store(self, w_cols, n, dst, scale=1.0, f32=False):
    P = self.P
    st, sk = (self.stgf() if f32 else self.stg())

    def evac(pp, ppk, t):
        P.op("act", "activation", r=[ppk], w=[f"{sk}.{t}"], out=st[0:n, t * 512:(t + 1) * 512], in_=pp[0:n, :], func=AF.Copy, scale=scale)
    self.proj_fm(w_cols, n, evac)
    P.dma("sp", sk, r=[f"{sk}.{t}" for t in range(NTILE)], w=[f"{sk}.{t}" for t in range(NTILE)] + ["dram." + dst.tensor.name],
          out=dst, in_=st[0:n, :])


def _stgf_get(self):
    if self._stgf is None:
        self._stgf = self.ring("stgf", 2, [128, NT], F32)
    return self._stgf()


KB.stgf = _stgf_get
KB.proj_setup = _proj_setup
KB.proj_fm = _proj_fm
KB.proj_fm_store = _proj_fm_store


def _fox_proj(self, w_in, b_f, qT, kT, v_tm, nlogf, qxT, parts="ab"):
    P = self.P
    if "a" in parts:
        for ch in range(6):
            self.proj_fm_store(w_in[:, ch * 128:(ch + 1) * 128], 128, qT[ch * 128:(ch + 1) * 128, :], scale=0.125)
        for ch in range(6):
            self.proj_fm_store(w_in[:, 768 + ch * 128:768 + (ch + 1) * 128], 128, kT[ch * 128:(ch + 1) * 128, :])
        for ch in range(2):
            self.proj_fm_store(w_in[:, 2316 + ch * 128:2316 + (ch + 1) * 128], 128, qxT[ch * 128:(ch + 1) * 128, :], scale=0.125)
    if "b" not in parts:
        return
    wv = self.sb("wv", [128, 8, 768], BF16)
    P.dma("pool", "wv", w=["wv"], out=wv[:], in_=w_in[:, 1536:2304].rearrange("(c p) n -> p c n", p=128))
    wf32 = self.sb("wf32", [128, 8, 12], F32)
    P.dma("sp", "wf32", w=["wf32"], out=wf32[:], in_=w_in[:, 2304:2316].rearrange("(c p) n -> p c n", p=128))
    wfb = self.sb("wfb", [128, 8, 12], BF16)
    P.op("dve", "tensor_copy", r=["wf32"], w=["wfb"], out=wfb[:], in_=wf32[:])
    bft = self.sb("bft", [128, 12], F32)
    P.dma("sp", "bft", w=["bft"], out=bft[:], in_=b_f.partition_broadcast(128))
    vst = self.ring("vst", 3, [128, 768], BF16)
    nlf = self.sb("nlf", [128, 16, 12], F32)
    zt = self.ring("zt", 2, [128, 12], F32)
    for tb in range(16):
        t = tb // 4
        ts_ = slice(tb * 128, (tb + 1) * 128)
        vs, vk = vst()
        for (c0, n) in ((0, 512), (512, 256)):
            ip = 4 + (self.psrr % 3)
            self.psrr += 1
            pp, ppk = self.ps[ip], self.psk[ip]
            for c in range(8):
                P.op("pe", "matmul", r=["wv", f"h{c}.{t}"], w=[ppk], out=pp[:, 0:n], lhsT=self.hT[c][:, ts_], rhs=wv[:, c, c0:c0 + n],
                     start=(c == 0), stop=(c == 7))
            if c0 == 0:
                P.op("act", "activation", r=[ppk], w=[vk + "a"], out=vs[:, 0:512], in_=pp[:, 0:512], func=AF.Copy)
            else:
                for c in range(8):
                    P.op("pe", "matmul", r=["wfb", f"h{c}.{t}"], w=[ppk], out=pp[:, 256:268], lhsT=self.hT[c][:, ts_], rhs=wfb[:, c, :],
                         start=(c == 0), stop=(c == 7))
                P.op("act", "activation", r=[ppk], w=[vk + "b"], out=vs[:, 512:768], in_=pp[:, 0:256], func=AF.Copy)
                z, zk = zt()
                P.op("dve", "tensor_tensor", r=[ppk, "bft"], w=[zk], out=z[:], in0=pp[:, 256:268], in1=bft[:], op=ALU.add)
                P.op("act", "activation", r=[zk], w=[zk], out=z[:], in_=z[:], func=AF.Exp, scale=-1.0)
                P.op("act", "activation", r=[zk, "one_col"], w=[f"nlf.{tb}"], out=nlf[:, tb, :], in_=z[:], func=AF.Ln, bias=self.one_col[:, 0:1])
        P.dma("sp", vk, r=[vk + "a", vk + "b"], w=[vk + "a", vk + "b", "dram.v"], out=v_tm[ts_, :], in_=vs[:])
    P.dma("sp", "nlf", r=[f"nlf.{tb}" for tb in range(16)], w=["dram.nlf"], out=nlogf, in_=nlf[:].rearrange("p b h -> p (b h)"))


KB.fox_proj = _fox_proj


def _attn_consts(self):
    P = self.P
    self.onesf = self.sb("onesf", [128, 128], F32)
    P.op("pool", "memset", w=["onesf"], ap=self.onesf[:], constant=1.0)
    self.trif = self.sb("trif", [128, 128], F32)
    P.op("pool", "affine_select", r=["onesf"], w=["trif"], out=self.trif[:], in_=self.onesf[:], pattern=[[1, 128]],
         compare_op=ALU.is_ge, fill=0.0, base=0, channel_multiplier=-1)
    self.identf = self.sb("identf", [128, 128], F32)
    P.op("pool", "affine_select", r=["onesf"], w=["identf"], out=self.identf[:], in_=self.onesf[:], pattern=[[1, 128]],
         compare_op=ALU.is_equal, fill=0.0, base=0, channel_multiplier=-1)
    self.ident_bf = self.sb("ident_bf", [128, 128], BF16)
    P.op("pool", "tensor_copy", r=["identf"], w=["ident_bf"], out=self.ident_bf[:], in_=self.identf[:])
    self.pt = self.ring("pt", 4, [128, 512], BF16)
    self.rz = self.ring("rz", 2, [128, 512], F32)


def _fox_attn(self, qT, kT_all, v_all, nlf_all, maskadd_d, selmask_d):
    P = self.P
    NL = self.sb("NL", [128, 128, 12], F32)
    NLc = self.sb("NLc", [128, 8, 192], F32)
    P.dma("sp", "NLc", w=["NLc"], out=NLc[:], in_=nlf_all.rearrange("c p f -> p c f"))
    P.op("pool", "tensor_copy", r=["NLc"], w=[f"NL.{cp}" for cp in range(8)], out=NL[:].rearrange("p (j c) h -> p j c h", c=8),
         in_=NLc[:].rearrange("p c (j h) -> p j c h", h=12))
    NLf = NL[:].rearrange("p b h -> p (b h)")
    TOT = self.sb("TOT", [128, 128, 12], F32)
    TOTf = TOT[:].rearrange("p b h -> p (b h)")
    CS = self.sb("CS", [128, 128, 12], F32)
    CSf = CS[:].rearrange("p b h -> p (b h)")
    negF = self.sb("negF", [128, 128, 12], F32)
    negFf = negF[:].rearrange("p b h -> p (b h)")
    nlk = [f"NL.{cp}" for cp in range(8)]
    for b in range(3):
        P.op("pe", "matmul", r=nlk + ["onesf"], w=[self.psk[3 + b]], out=self.ps[3 + b][:], lhsT=self.onesf[:], rhs=NLf[:, b * 512:(b + 1) * 512],
             start=True, stop=True)
        P.op("act", "activation", r=[self.psk[3 + b]], w=[f"TOT.{b}"], out=TOTf[:, b * 512:(b + 1) * 512], in_=self.ps[3 + b][:], func=AF.Copy)
    for b in range(3):
        P.op("pe", "matmul", r=nlk + ["trif"], w=[self.psk[b]], out=self.ps[b][:], lhsT=self.trif[:], rhs=NLf[:, b * 512:(b + 1) * 512],
             start=True, stop=True)
    for h in range(12):
        P.op("dve", "tensor_tensor_scan", r=["TOT.0", "TOT.1", "TOT.2", "onesf"], w=[f"CS.{h}"], out=CS[:, :, h], data0=self.onesf[:],
             data1=TOT[:, :, h], initial=0.0, op0=ALU.mult, op1=ALU.add)
    csk = [f"CS.{h}" for h in range(12)]
    P.op("dve", "tensor_copy", r=[self.psk[0]], w=["negF.a"], out=negFf[:, 0:12], in_=self.ps[0][:, 0:12])
    P.op("dve", "tensor_tensor", r=[self.psk[0]] + csk, w=["negF.0"], out=negFf[:, 12:512], in0=self.ps[0][:, 12:512], in1=CSf[:, 0:500], op=ALU.add)
    P.op("dve", "tensor_tensor", r=[self.psk[1]] + csk, w=["negF.1"], out=negFf[:, 512:1024], in0=self.ps[1][:], in1=CSf[:, 500:1012], op=ALU.add)
    P.op("dve", "tensor_tensor", r=[self.psk[2]] + csk, w=["negF.2"], out=negFf[:, 1024:1536], in0=self.ps[2][:], in1=CSf[:, 1012:1524], op=ALU.add)
    nfk = ["negF.a", "negF.0", "negF.1", "negF.2"]
    selm = self.sb("selm", [128, 1536], F32)
    P.dma("sp", "selm", w=["selm"], out=selm[:], in_=selmask_d)
    prod = self.sb("prod", [128, 1536], F32)
    P.op("dve", "tensor_tensor", r=nfk + ["selm"], w=["prod"], out=prod[:], in0=negFf, in1=selm[:], op=ALU.mult)
    Fql = self.sb("Fql", [128, 16, 12], F32)
    P.op("dve", "tensor_reduce", r=["prod"], w=["Fql"], out=Fql[:], in_=prod[:].rearrange("p (j c h) -> p j h c", j=16, c=8, h=12),
         axis=AX.X, op=ALU.add)
    Fq_rows = self.sb("Fq_rows", [12, NT], BF16)
    for T in range(4):
        pb, pk = self.ps[4 + T], self.psk[4 + T]
        for jj in range(4):
            j = 4 * T + jj
            P.op("pe", "transpose", r=["Fql", "identf"], w=[pk], out=pb[0:12, jj * 128:(jj + 1) * 128], in_=Fql[:, j, :], identity=self.identf[:])
        P.op("act", "activation", r=[pk], w=[f"Fq.{T}"], out=Fq_rows[:, T * 512:(T + 1) * 512], in_=pb[0:12, :], func=AF.Copy, scale=-1.0)
    fqk = [f"Fq.{T}" for T in range(4)]
    madd = self.sb("madd", [128, 8, 128], BF16)
    P.dma("pool", "madd", w=["madd"], out=madd[:], in_=maskadd_d.rearrange("p (b q) -> p b q", b=8))
    QA = self.ring("QA", 2, [65, NT], BF16)
    KA = self.ring("KA", 3, [65, 1024], BF16)
    VAe = self.ring("VAe", 2, [128, 8, 128], BF16)
    VAo = self.ring("VAo", 2, [128, 8, 128], BF16)
    ka_slots = [KA() for _ in range(3)]
    for kt, kk in ka_slots:
        P.op("pool", "memset", w=[kk + ".one"], ap=kt[64:65, :], constant=1.0)
    for ring_, lo in ((VAe, 64), (VAo, 0)):
        for _ in range(2):
            vt, vk = ring_()
            P.op("pool", "memset", w=[vk + ".one"], ap=vt[:, :, lo:lo + 64], constant=1.0)
    sti = [0]
    for h in range(12):
        even = (h % 2 == 0)
        qa, qk = QA()
        P.dma("sp", qk, w=[qk + ".q"], out=qa[0:64, :], in_=qT[h * 64:(h + 1) * 64, :])
        P.dma("sp", qk, r=fqk, w=[qk + ".f"], out=qa[64:65, :], in_=Fq_rows[h:h + 1, :])
        items = []
        for J in range(16):
            for T in range(4):
                jstart = max(J, 4 * T)
                if jstart > 4 * T + 3:
                    continue
                for kb in range(8):
                    items.append((J, T, kb))
        loaded = {}

        def mk(J, T, kb):
            st = {}
            jstart = max(J, 4 * T)
            c0 = (jstart - 4 * T) * 128
            ncol = 512 - c0
            diag = (J >= 4 * T)
            acc, acck = self.ps[T], self.psk[T]

            def front():
                if J not in loaded:
                    ka, kk = KA()
                    P.dma("sp", kk, w=[kk], out=ka[0:64, :].rearrange("d (c p) -> d c p", c=8),
                          in_=kT_all[:, h * 64:(h + 1) * 64, J * 128:(J + 1) * 128].rearrange("c d p -> d c p"))
                    va, vk = (VAe if even else VAo)()
                    vlo = 0 if even else 64
                    P.dma("sp", vk, w=[vk], out=va[:, :, vlo:vlo + 64],
                          in_=v_all[:, J * 128:(J + 1) * 128, h * 64:(h + 1) * 64].rearrange("c p d -> p c d"))
                    loaded[J] = (ka, kk, va, vk)
                ka, kk, va, vk = loaded[J]
                si = 4 + (sti[0] % 4)
                sti[0] += 1
                sp_, spk = self.ps[si], self.psk[si]
                st["sp"] = (sp_, spk)
                P.op("pe", "matmul", r=[kk, kk + ".one", qk + ".q", qk + ".f"], w=[spk], out=sp_[:, 0:ncol], lhsT=ka[:, kb * 128:(kb + 1) * 128],
                     rhs=qa[:, T * 512 + c0:(T + 1) * 512], start=True, stop=(not diag))
                if diag:
                    P.op("pe", "matmul", r=["ident_bf", "madd"], w=[spk], out=sp_[:, 0:128], lhsT=self.ident_bf[:], rhs=madd[:, kb, :],
                         start=False, stop=True)

            def back():
                ka, kk, va, vk = loaded[J]
                sp_, spk = st["sp"]
                pt, ptk = self.pt()
                P.op("act", "activation", r=[spk] + nfk, w=[ptk], out=pt[:, 0:ncol], in_=sp_[:, 0:ncol], func=AF.Exp,
                     bias=negF[:, 8 * J + kb, h:h + 1])
                first = (J == 0 and kb == 0)
                last = (J == 4 * T + 3 and kb == 7)
                P.op("pe", "matmul", r=[ptk, vk, vk + ".one"], w=[acck], out=acc[:, c0:512], lhsT=va[:, kb, :], rhs=pt[:, 0:ncol],
                     start=first, stop=last)
            return (front, back)
        _pipeline([mk(*it) for it in items], L=2)
        for T in range(4):
            acc, acck = self.ps[T], self.psk[T]
            rz, rzk = self.rz()
            cs = slice(T * 512, (T + 1) * 512)
            nlo, zlo = (0, 64) if even else (64, 0)
            P.op("dve", "reciprocal", r=[acck], w=[rzk], out=rz[zlo:zlo + 64, :], in_=acc[zlo:zlo + 64, :])
            P.op("dve", "tensor_tensor", r=[acck, rzk], w=[f"h{h // 2}.{T}"], out=self.hT[h // 2][nlo:nlo + 64, cs], in0=acc[nlo:nlo + 64, :],
                 in1=rz[zlo:zlo + 64, :], op=ALU.mult)


def _pipeline(items, L=2):
    n = len(items)
    for i in range(n + L):
        if i < n:
            items[i][0]()
        if i >= L:
            items[i - L][1]()


KB.attn_consts = _attn_consts
KB.fox_attn = _fox_attn


def _mem_attn(self, memT_d, mem_norm, w_kv, qxT):
    P = self.P
    mT = [self.sb(f"mT{c}", [128, 256], F32) for c in range(8)]
    mhT = [self.sb(f"mhT{c}", [128, 256], BF16) for c in range(8)]
    for c in range(8):
        P.dma("sp", f"mT{c}", w=[f"m{c}.0"], out=mT[c][:], in_=memT_d[c])
    self.rmsnorm(mem_norm, ntile=1, xT=mT, hT=mhT, xk="m", hk="mh", width=256)
    wkv = self.sb("wkv", [128, 8, 512], BF16)
    P.dma("pool", "wkv", w=["wkv"], out=wkv[:], in_=w_kv.rearrange("(c p) n -> p c n", p=128))
    mK = self.sb("mK", [64, 4, 256], BF16)
    mV = [self.sb(f"mV{hm}", [128, 2, 128], BF16) for hm in range(4)]
    mhk = [f"mh{c}.0" for c in range(8)]
    for hm in range(4):
        pb, pk = self.ps[4 + hm], self.psk[4 + hm]
        for c in range(8):
            P.op("pe", "matmul", r=["wkv"] + mhk, w=[pk], out=pb[0:64, 0:256], lhsT=wkv[:, c, hm * 64:(hm + 1) * 64], rhs=mhT[c][:, :],
                 start=(c == 0), stop=(c == 7))
        P.op("act", "activation", r=[pk], w=["mK"], out=mK[:, hm, :], in_=pb[0:64, 0:256], func=AF.Copy)
        lo = 64 if hm % 2 == 0 else 0
        P.op("pool", "memset", w=[f"mV{hm}.one"], ap=mV[hm][:, :, lo:lo + 64], constant=1.0)
    for mb in range(2):
        pb, pk = self.ps[mb], self.psk[mb]
        for c in range(8):
            P.op("pe", "matmul", r=["wkv"] + mhk, w=[pk], out=pb[:, 0:256], lhsT=mhT[c][:, mb * 128:(mb + 1) * 128], rhs=wkv[:, c, 256:512],
                 start=(c == 0), stop=(c == 7))
        for hm in range(4):
            lo = 0 if hm % 2 == 0 else 64
            P.op("act", "activation", r=[pk], w=[f"mV{hm}.v"], out=mV[hm][:, mb, lo:lo + 64], in_=pb[:, hm * 64:(hm + 1) * 64], func=AF.Copy)
    qx = self.sb("qx", [64, 4, NT], BF16)
    P.dma("sp", "qx", w=["qx"], out=qx[:], in_=qxT.rearrange("(h d) n -> d h n", h=4))
    sti = 0
    for hm in range(4):
        even = hm % 2 == 0
        for T in range(4):
            acc, acck = self.ps[T], self.psk[T]
            cs = slice(T * 512, (T + 1) * 512)
            for mb in range(2):
                si = 4 + (sti % 4)
                sti += 1
                sp_, spk = self.ps[si], self.psk[si]
                P.op("pe", "matmul", r=["mK", "qx"], w=[spk], out=sp_[:], lhsT=mK[:, hm, mb * 128:(mb + 1) * 128], rhs=qx[:, hm, cs], start=True, stop=True)
                pt, ptk = self.pt()
                P.op("act", "activation", r=[spk], w=[ptk], out=pt[:], in_=sp_[:], func=AF.Exp)
                P.op("pe", "matmul", r=[ptk, f"mV{hm}.v", f"mV{hm}.one"], w=[acck], out=acc[:], lhsT=mV[hm][:, mb, :], rhs=pt[:], start=(mb == 0), stop=(mb == 1))
            rz, rzk = self.rz()
            nlo, zlo = (0, 64) if even else (64, 0)
            P.op("dve", "reciprocal", r=[acck], w=[rzk], out=rz[zlo:zlo + 64, :], in_=acc[zlo:zlo + 64, :])
            P.op("dve", "tensor_tensor", r=[acck, rzk], w=[f"h{6 + hm // 2}.{T}"], out=self.hT[6 + hm // 2][nlo:nlo + 64, cs], in0=acc[nlo:nlo + 64, :],
                 in1=rz[zlo:zlo + 64, :], op=ALU.mult)


def _mix_out(self, w_o):
    P = self.P
    wo = self.sb("wmo", [128, 8, D], BF16)
    P.dma("pool", "wmo", w=["wmo"], out=wo[:], in_=w_o.rearrange("(k p) n -> p k n", p=128))
    for c in range(8):
        for T in range(4):
            cs = slice(T * 512, (T + 1) * 512)
            ip = 4 + (self.psrr % 4)
            self.psrr += 1
            po, pok = self.ps[ip], self.psk[ip]
            for kc in range(8):
                P.op("pe", "matmul", r=["wmo", f"h{kc}.{T}"], w=[pok], out=po[:], lhsT=wo[:, kc, c * 128:(c + 1) * 128], rhs=self.hT[kc][:, cs],
                     start=(kc == 0), stop=(kc == 7))
            P.op("dve", "tensor_tensor", r=[pok, f"x{c}.{T}"], w=[f"x{c}.{T}"], out=self.xT[c][:, cs], in0=po[:], in1=self.xT[c][:, cs], op=ALU.add)


KB.mem_attn = _mem_attn
KB.mix_out = _mix_out


def _conv_proj(self, w_in, uT, bgT, qxT, hsrc=None):
    P = self.P
    cgs = self.ring("cgs", 2, [128, NT], F32)
    for ch in range(6):
        self.proj_fm_store(w_in[:, ch * 128:(ch + 1) * 128], 128, bgT[ch * 128:(ch + 1) * 128, :], f32=True)
    for ch in range(6):
        cg, cgk = cgs()

        def evac_c(pp, ppk, t, cg=cg, cgk=cgk):
            P.op("act", "activation", r=[ppk], w=[f"{cgk}.{t}"], out=cg[:, t * 512:(t + 1) * 512], in_=pp[:], func=AF.Copy)
        self.proj_fm(w_in[:, 768 + ch * 128:768 + (ch + 1) * 128], 128, evac_c)
        st, sk = self.stgf()

        def evac_v(pp, ppk, t, cg=cg, cgk=cgk, st=st, sk=sk):
            P.op("dve", "tensor_tensor", r=[ppk, f"{cgk}.{t}"], w=[f"{sk}.{t}"], out=st[:, t * 512:(t + 1) * 512], in0=pp[:],
                 in1=cg[:, t * 512:(t + 1) * 512], op=ALU.mult)
        self.proj_fm(w_in[:, 1536 + ch * 128:1536 + (ch + 1) * 128], 128, evac_v)
        if hsrc is not None:
            if not hasattr(self, "_hs"):
                self._hs = self.ring("hs", 2, [128, 16, 2], F32)
            hs, hsk = self._hs()
            P.op("pool", "tensor_copy", r=[f"{sk}.{t}" for t in range(NTILE)], w=[hsk], out=hs[:],
                 in_=st[:].rearrange("p (j q) -> p j q", q=128)[:, :, 126:128])
            P.dma("sp", hsk, r=[hsk], w=[hsk, "dram.hsrc"], out=hsrc[ch * 128:(ch + 1) * 128, :], in_=hs[:].rearrange("p j t -> p (j t)"))
        P.dma("sp", sk, r=[f"{sk}.{t}" for t in range(NTILE)], w=[f"{sk}.{t}" for t in range(NTILE)] + ["dram.u"],
              out=uT[ch * 128:(ch + 1) * 128, :], in_=st[:])
    for ch in range(2):
        self.proj_fm_store(w_in[:, 2304 + ch * 128:2304 + (ch + 1) * 128], 128, qxT[ch * 128:(ch + 1) * 128, :], scale=0.125)


def _conv_mix(self, uT, halo, bgT, conv_w, hall=None, halosel_d=None):
    P = self.P
    wc = self.sb("wc", [128, 3, 6], F32)
    P.dma("sp", "wc", w=["wc.0", "wc.1", "wc.2"], out=wc[:], in_=conv_w.rearrange("p (k c) -> p k c", k=3))
    U = self.ring("U", 2, [128, 16, 130], F32)
    BG = self.ring("BG", 2, [128, 16, 128], F32)
    AC = self.ring("AC", 2, [128, 16, 128], F32)
    for ch in range(6):
        u, uk = U()
        bg, bk = BG()
        ac, ak = AC()
        rows = slice(ch * 128, (ch + 1) * 128)
        P.dma("sp", uk, w=[uk + ".u"], out=u[:, :, 2:130], in_=uT[rows, :].rearrange("p (j q) -> p j q", q=128))
        if hall is None:
            P.dma("sp", uk, w=[uk + ".h"], out=u[:, :, 0:2], in_=halo[rows, :, :])
        else:
            hc, hk = self.conv_halo_select(hall, halosel_d, ch, u)
            P.op("dve", "tensor_reduce", r=[hk + ".p"], w=[uk + ".h"], out=u[:, :, 0:2], in_=hc[:].rearrange("p c j t -> p j t c"), axis=AX.X, op=ALU.add)
        P.dma("sp", bk, w=[bk], out=bg[:], in_=bgT[rows, :].rearrange("p (j q) -> p j q", q=128))
        uks = [uk + ".u", uk + ".h"]
        wck = ["wc.0", "wc.1", "wc.2"]
        P.op("dve", "tensor_scalar", r=uks + wck, w=[ak], out=ac[:], in0=u[:, :, 2:130], scalar1=wc[:, 2, ch:ch + 1], scalar2=None, op0=ALU.mult)
        P.op("dve", "scalar_tensor_tensor", r=uks + wck + [ak], w=[ak], out=ac[:], in0=u[:, :, 1:129], scalar=wc[:, 1, ch:ch + 1], in1=ac[:],
             op0=ALU.mult, op1=ALU.add)
        P.op("dve", "scalar_tensor_tensor", r=uks + wck + [ak], w=[ak], out=ac[:], in0=u[:, :, 0:128], scalar=wc[:, 0, ch:ch + 1], in1=ac[:],
             op0=ALU.mult, op1=ALU.add)
        P.op("pool", "tensor_tensor", r=[ak, bk], w=[f"h{ch}.{t}" for t in range(NTILE)], out=self.hT[ch][:].rearrange("p (j q) -> p j q", q=128),
             in0=ac[:], in1=bg[:], op=ALU.mult)


KB.conv_proj = _conv_proj
KB.conv_mix = _conv_mix


ROPE_THETA = 500000.0
TWO_PI = float(2 * np.pi)


def host_rope_cols():
    c = np.zeros((128, 2), np.float32)
    for p in range(128):
        d = p % 64
        if d < 16:
            c[p, 0] = ROPE_THETA ** (-(d % 8) / 8.0)
            c[p, 1] = -1.0 if d < 8 else 1.0
    return c


def host_swap_cols(w, ncols):
    idx = np.arange(ncols)
    d = idx % 64
    src = np.where(d < 8, idx + 8, np.where(d < 16, idx - 8, idx))
    return np.ascontiguousarray(w[:, src])


def _rope_tables(self, pos_d, ropec_d):
    P = self.P
    rc = self.sb("ropec", [128, 2], F32)
    P.dma("sp", "ropec", w=["ropec"], out=rc[:], in_=ropec_d)
    ang = self.sb("ang", [128, NT], F32)
    P.dma("sp", "ang", w=["ang"], out=ang[:], in_=pos_d.partition_broadcast(128))
    P.op("dve", "tensor_scalar", r=["ang", "ropec"], w=["ang"], out=ang[:], in0=ang[:], scalar1=rc[:, 0:1], scalar2=None, op0=ALU.mult)
    I32 = mybir.dt.int32
    C1 = 6.28125
    C2 = float(2 * np.pi - 6.28125)
    PI = float(np.pi)
    ki = self.sb("rki", [128, NT], I32)
    r = self.sb("rr", [128, NT], F32)
    m = self.sb("rm", [128, NT], F32)
    self.Ck = self.sb("Ck", [128, NT], F32)
    self.Sk = self.sb("Sk", [128, NT], F32)
    P.op("dve", "tensor_scalar", r=["ang"], w=["rki"], out=ki[:], in0=ang[:], scalar1=float(1.0 / (2 * np.pi)), scalar2=None, op0=ALU.mult)
    P.op("dve", "tensor_copy", r=["rki"], w=["rr"], out=r[:], in_=ki[:])
    P.op("dve", "scalar_tensor_tensor", r=["rr", "ang"], w=["rm"], out=m[:], in0=r[:], scalar=-C1, in1=ang[:], op0=ALU.mult, op1=ALU.add)
    P.op("dve", "scalar_tensor_tensor", r=["rr", "rm"], w=["rr"], out=r[:], in0=r[:], scalar=-C2, in1=m[:], op0=ALU.mult, op1=ALU.add)
    P.op("dve", "tensor_scalar", r=["rr"], w=["rm"], out=m[:], in0=r[:], scalar1=PI, scalar2=None, op0=ALU.is_gt)
    P.op("dve", "scalar_tensor_tensor", r=["rr", "rm"], w=["rm"], out=m[:], in0=m[:], scalar=-TWO_PI, in1=r[:], op0=ALU.mult, op1=ALU.add)
    P.op("dve", "tensor_scalar", r=["rm"], w=["rm"], out=m[:], in0=m[:], scalar1=-PI, scalar2=PI, op0=ALU.max, op1=ALU.min)
    P.op("act", "activation", r=["rm"], w=["rm"], out=m[:], in_=m[:], func=AF.Sin)
    P.op("dve", "tensor_scalar", r=["rm", "ropec"], w=["Sk"], out=self.Sk[:], in0=m[:], scalar1=rc[:, 1:2], scalar2=None, op0=ALU.mult)
    P.op("dve", "tensor_scalar", r=["rr", "Sk"], w=["rr"], out=r[:], in0=r[:], scalar1=float(np.pi / 2), scalar2=None, op0=ALU.add)
    P.op("dve", "tensor_scalar", r=["rr"], w=["rm"], out=m[:], in0=r[:], scalar1=PI, scalar2=None, op0=ALU.is_gt)
    P.op("dve", "scalar_tensor_tensor", r=["rr", "rm"], w=["rm"], out=m[:], in0=m[:], scalar=-TWO_PI, in1=r[:], op0=ALU.mult, op1=ALU.add)
    P.op("dve", "tensor_scalar", r=["rm"], w=["rm"], out=m[:], in0=m[:], scalar1=-PI, scalar2=PI, op0=ALU.max, op1=ALU.min)
    P.op("act", "activation", r=["rm"], w=["Ck"], out=self.Ck[:], in_=m[:], func=AF.Sin)


def _proj_rope_store(self, w_cols, wsw_cols, dst, scale):
    Ct, St, ck, sk_ = self.Ck, self.Sk, "Ck", "Sk"
    P = self.P
    st, sk = self.stg()
    tA = self.ropeA

    def evacA(pp, ppk, t):
        ta, tak = tA()
        self._last_ta = (ta, tak)
        P.op("dve", "tensor_tensor", r=[ppk, ck], w=[tak], out=ta[:], in0=pp[:], in1=Ct[:, t * 512:(t + 1) * 512], op=ALU.mult)
        self._tas[t] = (ta, tak)
    self._tas = {}
    self.proj_fm(w_cols, 128, evacA)

    def evacB(pp, ppk, t):
        ta, tak = self._tas[t]
        tb, tbk = self.ropeB()
        P.op("dve", "tensor_tensor", r=[ppk, sk_], w=[tbk], out=tb[:], in0=pp[:], in1=St[:, t * 512:(t + 1) * 512], op=ALU.mult)
        P.op("pool", "tensor_tensor", r=[tak, tbk], w=[tbk], out=tb[:], in0=ta[:], in1=tb[:], op=ALU.add)
        P.op("act", "activation", r=[tbk], w=[f"{sk}.{t}"], out=st[:, t * 512:(t + 1) * 512], in_=tb[:], func=AF.Copy, scale=scale)
    self.proj_fm(wsw_cols, 128, evacB)
    P.dma("sp", sk, r=[f"{sk}.{t}" for t in range(NTILE)], w=[f"{sk}.{t}" for t in range(NTILE)] + ["dram." + dst.tensor.name], out=dst, in_=st[:])


def _nsa_proj(self, w_in, w_sw, pos_d, ropec_d, qT, kcT, vcT, ksT, vs_tm, kwT, vw_tm, sgT, qxT):
    P = self.P
    self.rope_tables(pos_d, ropec_d)
    self.ropeA = self.ring("ropeA", 5, [128, 512], F32)
    self.ropeB = self.ring("ropeB", 3, [128, 512], F32)
    for ch in range(6):
        self.proj_rope_store(w_in[:, ch * 128:(ch + 1) * 128], w_sw[:, ch * 128:(ch + 1) * 128], qT[ch * 128:(ch + 1) * 128, :], 0.125)
    for i, (off, dst) in enumerate(((768, kcT), (1280, ksT), (1792, kwT))):
        for ch in range(2):
            self.proj_rope_store(w_in[:, off + ch * 128:off + (ch + 1) * 128], w_sw[:, 768 + i * 256 + ch * 128:768 + i * 256 + (ch + 1) * 128],
                                 dst[ch * 128:(ch + 1) * 128, :], 1.0)
    for ch in range(2):
        self.proj_fm_store(w_in[:, 1024 + ch * 128:1024 + (ch + 1) * 128], 128, vcT[ch * 128:(ch + 1) * 128, :])
    for ch in range(2):
        self.proj_fm_store(w_in[:, 2340 + ch * 128:2340 + (ch + 1) * 128], 128, qxT[ch * 128:(ch + 1) * 128, :], scale=0.125)
    wg32 = self.sb("wg32", [128, 8, 36], F32)
    P.dma("sp", "wg32", w=["wg32"], out=wg32[:], in_=w_in[:, 2304:2340].rearrange("(c p) n -> p c n", p=128))
    wgb = self.sb("wgb", [128, 8, 36], BF16)
    P.op("dve", "tensor_copy", r=["wg32"], w=["wgb"], out=wgb[:], in_=wg32[:])
    sg = self.sb("sgst", [36, NT], F32)
    for t in range(NTILE):
        cs = slice(t * 512, (t + 1) * 512)
        ip = 4 + (self.psrr % 3)
        self.psrr += 1
        pp, ppk = self.ps[ip], self.psk[ip]
        for c in range(8):
            P.op("pe", "matmul", r=["wgb", f"h{c}.{t}"], w=[ppk], out=pp[0:36, :], lhsT=wgb[:, c, :], rhs=self.hT[c][:, cs], start=(c == 0), stop=(c == 7))
        P.op("act", "activation", r=[ppk], w=[f"sg.{t}"], out=sg[:, cs], in_=pp[0:36, :], func=AF.Sigmoid)
    P.dma("sp", "sgst", r=[f"sg.{t}" for t in range(NTILE)], w=["dram.sg"], out=sgT, in_=sg[:])
    wv = self.sb("wvsw", [128, 8, 512], BF16)
    P.dma("pool", "wvsw", w=["wvsw.a"], out=wv[:, :, 0:256], in_=w_in[:, 1536:1792].rearrange("(c p) n -> p c n", p=128))
    P.dma("pool", "wvsw", w=["wvsw.b"], out=wv[:, :, 256:512], in_=w_in[:, 2048:2304].rearrange("(c p) n -> p c n", p=128))
    vst = self.ring("vst2", 3, [128, 512], BF16)
    for tb in range(16):
        t = tb // 4
        ts_ = slice(tb * 128, (tb + 1) * 128)
        vs, vk = vst()
        ip = 4 + (self.psrr % 3)
        self.psrr += 1
        pp, ppk = self.ps[ip], self.psk[ip]
        for c in range(8):
            P.op("pe", "matmul", r=["wvsw.a", "wvsw.b", f"h{c}.{t}"], w=[ppk], out=pp[:], lhsT=self.hT[c][:, ts_], rhs=wv[:, c, :], start=(c == 0), stop=(c == 7))
        P.op("act", "activation", r=[ppk], w=[vk], out=vs[:], in_=pp[:], func=AF.Copy)
        P.dma("sp", vk, r=[vk], w=[vk, "dram.vs"], out=vs_tm[ts_, :], in_=vs[:, 0:256])
        P.dma("sp", vk + "w", r=[vk], w=[vk, "dram.vw"], out=vw_tm[ts_, :], in_=vs[:, 256:512])


KB.rope_tables = _rope_tables
KB.proj_rope_store = _proj_rope_store
KB.nsa_proj = _nsa_proj


BIGF = 1.0e4


def host_ovA():
    o = np.zeros((128, 8, 257), np.float32)
    for m in range(8):
        for pn in range(128):
            n = 128 * m - 1 + pn
            if n < 0 or n > 1022:
                continue
            for j in range(256):
                if 16 * n < 64 * j + 64 and 16 * n + 32 > 64 * j:
                    o[pn, m, j] = 1.0
            o[pn, m, 256] = 1.0
    return o.reshape(128, 8 * 257)


def host_gsel():
    g = np.zeros((100, 36, 128), np.float32)
    for r in range(36):
        g[r, r, :] = 1.0
        g[64 + r, r, :] = 1.0
    return g.reshape(100, 36 * 128)


def host_epat():
    e = np.zeros((4, 64, 1024), np.float32)
    for jm in range(4):
        for i in range(16):
            e[jm, 16 * jm + i, 64 * i:64 * (i + 1)] = -NEG
    return e


def host_wmask():
    ps = np.arange(128)[:, None]
    pq = np.arange(128)[None, :]
    w = np.zeros((128, 2, 128), np.float32)
    w[:, 0, :] = np.where(ps > pq, 0.0, NEG)
    w[:, 1, :] = np.where(ps <= pq, 0.0, NEG)
    return w.reshape(128, 256)


def host_cmask(c):
    pn = np.arange(128)[:, None]
    pq = np.arange(128)[None, :]
    m = np.zeros((128, 2, 128), np.float32)
    m[:, 0, :] = np.where(16 * pn + 15 <= 128 * c + pq, 0.0, NEG)
    m[:, 1, :] = np.where(16 * pn - 1009 <= 128 * c + pq, 0.0, NEG)
    return m.reshape(128, 256)


def host_adj(c):
    a = np.zeros((16, 128, 256), np.float32)
    js = np.arange(256)[None, :]
    for j in range(16):
        tblk = (16 * j + 2 * c + (np.arange(128) >= 64).astype(np.int64))[:, None]
        forced = (js == 0) | (js == tblk) | (js == tblk - 1)
        valid = js <= tblk
        a[j] = np.where(valid, np.where(forced, BIGF, 0.0), -BIGF)
    return a


def host_kvwin(c, kwT_all, vw_all):
    dt = kwT_all.dtype
    kwin = np.zeros((16, 4, 65, 640), dt)
    vwin = np.zeros((16, 640, 256), dt)
    for j in range(16):
        gb = 8 * j + c
        for bi in range(5):
            b = gb - 4 + bi
            if b < 0:
                kwin[j, :, 64, bi * 128:(bi + 1) * 128] = NEG
                continue
            cc, jj = b % 8, b // 8
            kwin[j, :, 0:64, bi * 128:(bi + 1) * 128] = kwT_all[cc][:, jj * 128:(jj + 1) * 128].reshape(4, 64, 128)
            vwin[j, bi * 128:(bi + 1) * 128, :] = vw_all[cc][jj * 128:(jj + 1) * 128, :]
    return kwin, vwin


def _nsa_compress(self, kcT_all, vcT_all, cmp_pos, cmp_w1, cmp_w2):
    P = self.P
    self.kcmpT = self.sb("kcmpT", [64, 4, 1024], BF16)
    self.vcA = self.sb("vcA", [128, 4, 8, 128], BF16)
    P.op("pool", "memset", w=["vcA.one"], ap=self.vcA[:, :, :, 64:128], constant=1.0)
    P.op("pool", "memset", r=["vcA.one"], w=["vcA.one"], ap=self.vcA[0:1, :, 0, 64:128], constant=0.0)
    P.op("pool", "memset", w=["kcmpT.0"], ap=self.kcmpT[:, :, 0:1], constant=0.0)
    with self.scope():
        full = self.ring("cfull", 2, [64, S], BF16)
        h1r = self.ring("h1T", 2, [64, 1024], BF16)
        for kv, src in ((0, kcT_all), (1, vcT_all)):
            w1 = self.sb(f"cw1_{kv}", [64, 32, 64], BF16)
            P.dma("pool", f"cw1_{kv}", w=[f"cw1_{kv}"], out=w1[:], in_=cmp_w1[kv].rearrange("(l d) h -> d l h", d=64))
            w2 = self.sb(f"cw2_{kv}", [64, 64], BF16)
            P.dma("pool", f"cw2_{kv}", w=[f"cw2_{kv}"], out=w2[:], in_=cmp_w2[kv])
            pe32 = self.sb(f"cpe32_{kv}", [64, 32], F32)
            P.dma("sp", f"cpe_{kv}", w=[f"cpe32_{kv}"], out=pe32[:], in_=cmp_pos[kv])
            peb = self.sb(f"cpeb_{kv}", [64, 32], BF16)
            P.op("dve", "tensor_copy", r=[f"cpe32_{kv}"], w=[f"cpeb_{kv}"], out=peb[:], in_=pe32[:])
            b1 = self.sb(f"cb1_{kv}", [64, 1], F32)
            pb, pk = self.ps[7], self.psk[7]
            for l in range(32):
                P.op("pe", "matmul", r=[f"cw1_{kv}", f"cpeb_{kv}"], w=[pk], out=pb[0:64, 0:1], lhsT=w1[:, l, :], rhs=peb[:, l:l + 1], start=(l == 0), stop=(l == 31))
            P.op("dve", "tensor_copy", r=[pk], w=[f"cb1_{kv}"], out=b1[:], in_=pb[0:64, 0:1])
            for g in range(4):
                fu, fk = full()
                fv = fu[:].rearrange("d (j c p) -> d j c p", j=16, c=8, p=128)
                for cp in range(8):
                    P.dma("sp", fk, w=[f"{fk}.{cp}"], out=fv[:, :, cp, :], in_=src[cp, g * 64:(g + 1) * 64, :].rearrange("d (j p) -> d j p", p=128))
                fks = [f"{fk}.{cp}" for cp in range(8)]
                h1, hk = h1r()
                P.op("pool", "memset", w=[hk + ".z"], ap=h1[:, 0:1], constant=0.0)
                for (n0, ncols) in ((0, 512), (512, 511)):
                    ip = 4 + (self.psrr % 3)
                    self.psrr += 1
                    pp, ppk = self.ps[ip], self.psk[ip]
                    for l in range(32):
                        a = l + 16 * n0
                        P.op("pe", "matmul", r=fks + [f"cw1_{kv}"], w=[ppk], out=pp[0:64, 0:ncols], lhsT=w1[:, l, :],
                             rhs=fu[:, a:a + 16 * (ncols - 1) + 1:16], start=(l == 0), stop=(l == 31))
                    P.op("act", "activation", r=[ppk, f"cb1_{kv}"], w=[f"{hk}.{n0}"], out=h1[:, 1 + n0:1 + n0 + ncols], in_=pp[0:64, 0:ncols],
                         func=AF.Silu, bias=b1[:, 0:1])
                hks = [hk + ".z", f"{hk}.0", f"{hk}.512"]
                if kv == 0:
                    for half in range(2):
                        ip = 4 + (self.psrr % 3)
                        self.psrr += 1
                        pp, ppk = self.ps[ip], self.psk[ip]
                        P.op("pe", "matmul", r=hks + [f"cw2_{kv}"], w=[ppk], out=pp[0:64, :], lhsT=w2[:], rhs=h1[:, half * 512:(half + 1) * 512], start=True, stop=True)
                        lo = 1 if half == 0 else 0
                        P.op("act", "activation", r=[ppk], w=[f"kcmpT.{g}.{half}"], out=self.kcmpT[:, g, half * 512 + lo:(half + 1) * 512],
                             in_=pp[0:64, lo:512], func=AF.Copy)
                else:
                    for m in range(8):
                        ip = 4 + (self.psrr % 3)
                        self.psrr += 1
                        pp, ppk = self.ps[ip], self.psk[ip]
                        P.op("pe", "matmul", r=hks + [f"cw2_{kv}"], w=[ppk], out=pp[:, 0:64], lhsT=h1[:, m * 128:(m + 1) * 128], rhs=w2[:], start=True, stop=True)
                        P.op("act", "activation", r=[ppk], w=[f"vcA.{g}.{m}"], out=self.vcA[:, g, m, 0:64], in_=pp[:, 0:64], func=AF.Copy)


KB.nsa_compress = _nsa_compress


def _nsa_attn(self, qT, ksT_all, vs_all, kwin_d, vwin_d, sgT, adj_d, cmask_d, maskadd_d, ovA_d, gsel_d, epat_d, wmask_d,
              kwT_all=None, vw_all=None, wmask16_d=None):
    P = self.P
    ovA = self.sb("ovA", [128, 8, 257], BF16)
    P.dma("pool", "ovA", w=["ovA"], out=ovA[:], in_=ovA_d.rearrange("p (m j) -> p m j", m=8))
    gsel = self.sb("gsel", [100, 36, 128], BF16)
    P.dma("pool", "gsel", w=["gsel"], out=gsel[:], in_=gsel_d.rearrange("p (r q) -> p r q", r=36))
    madd = self.sb("madd", [128, 8, 128], BF16)
    P.dma("pool", "madd", w=["madd"], out=madd[:], in_=maskadd_d.rearrange("p (b q) -> p b q", b=8))
    cmask = self.sb("cmask", [128, 2, 128], BF16)
    P.dma("pool", "cmask", w=["cmask"], out=cmask[:], in_=cmask_d.rearrange("p (b q) -> p b q", b=2))
    uniform_w = kwT_all is not None
    if uniform_w:
        wmask = self.sb("wmask16", [128, 12, 128], BF16)
        P.dma("pool", "wmask", w=["wmask"], out=wmask[:], in_=wmask16_d.rearrange("p (b q) -> p b q", b=12))
    else:
        wmask = self.sb("wmask", [128, 2, 128], BF16)
        P.dma("pool", "wmask", w=["wmask"], out=wmask[:], in_=wmask_d.rearrange("p (b q) -> p b q", b=2))
    sgh = self.sb("sgh", [100, NT], BF16)
    P.op("pool", "memset", w=["sgh.z"], ap=sgh[:], constant=0.0)
    with self.scope():
        sg32 = self.sb("sg32", [36, NT], F32)
        P.dma("sp", "sg32", w=["sg32"], out=sg32[:], in_=sgT)
        P.op("act", "activation", r=["sg32", "sgh.z"], w=["sgh.hi"], out=sgh[0:36, :], in_=sg32[:], func=AF.Copy)
        P.op("dve", "tensor_tensor", r=["sg32", "sgh.hi"], w=["sgh.lo"], out=sgh[64:100, :], in0=sg32[:], in1=sgh[0:36, :], op=ALU.subtract)
    sghk = ["sgh.z", "sgh.hi", "sgh.lo"]
    QA = self.sb("QA", [128, 4, 3, 512], BF16)
    QW = self.sb("QW", [65, 3, 512], BF16)
    P.op("pool", "memset", w=["QW.one"], ap=QW[64:65, :, :], constant=1.0)
    SELT = self.sb("SELT", [128, 4, 128], BF16)
    P.op("pool", "memset", w=["SELT.z"], ap=SELT[:], constant=0.0)
    KS = self.ring("KS", 3, [128, 1024], BF16)
    VS = self.ring("VS", 2, [128, 8, 128], BF16)
    for _ in range(2):
        vt, vk = VS()
        P.op("pool", "memset", w=[vk + ".one"], ap=vt[:, :, 64:128], constant=1.0)
    KW = self.ring("KW", 2, [65, 2048 if uniform_w else 640], BF16)
    VW = self.ring("VW", 2, [128, 16 if uniform_w else 5, 128], BF16)
    for _ in range(2):
        vt, vk = VW()
        P.op("pool", "memset", w=[vk + ".one"], ap=vt[:, :, 64:128], constant=1.0)
    ysum = [self.sb(f"ysum{i}", [64, 512], F32) for i in range(3)]
    impacc = self.sb("impacc", [128, 4, 256], F32)
    adjr = self.ring("adj", 2, [128, 256], F32)
    scr = self.ring("scr", 2, [128, 256], F32)
    m8 = self.ring("m8", 2, [128, 16], F32)
    coefr = self.ring("coef", 1, [128, 512], F32)
    ctr = self.ring("ctr", 1, [64, 512], F32)
    rzq = self.ring("rzq", 4, [128, 1], F32)
    tps = self.ps[7][:].bitcast(BF16)
    gps, gpk = self.ps[6], self.psk[6]
    sti = [0]

    def st_ring(banks):
        si = banks[sti[0] % len(banks)]
        sti[0] += 1
        return self.ps[si], self.psk[si]

    def finish_branch(acc, acck, b, h, hi_, T, first):
        cs = slice(T * 512, (T + 1) * 512)
        rz, rzk = self.rz()
        P.op("dve", "tensor_scalar", r=[acck], w=[rzk], out=rz[64:128, :], in0=acc[64:128, :], scalar1=1e-30, scalar2=None, op0=ALU.max)
        P.op("dve", "reciprocal", r=[rzk], w=[rzk], out=rz[64:128, :], in_=rz[64:128, :])
        P.op("pe", "matmul", r=["gsel"] + sghk, w=[gpk], out=gps[:], lhsT=gsel[:, b * 12 + h, :], rhs=sgh[:, cs], start=True, stop=True)
        co, cok = coefr()
        P.op("dve", "tensor_tensor", r=[rzk, gpk], w=[cok], out=co[64:128, :], in0=rz[64:128, :], in1=gps[64:128, :], op=ALU.mult)
        if first:
            P.op("dve", "tensor_tensor", r=[acck, cok], w=[f"ysum{hi_}"], out=ysum[hi_][:], in0=acc[0:64, :], in1=co[64:128, :], op=ALU.mult)
        else:
            ct, ctk = ctr()
            P.op("dve", "tensor_tensor", r=[acck, cok], w=[ctk], out=ct[:], in0=acc[0:64, :], in1=co[64:128, :], op=ALU.mult)
            P.op("pool", "tensor_tensor", r=[ctk, f"ysum{hi_}"], w=[f"ysum{hi_}"], out=ysum[hi_][:], in0=ct[:], in1=ysum[hi_][:], op=ALU.add)

    for g in range(4):
        for T in range(4):
            cs = slice(T * 512, (T + 1) * 512)
            qsrc = qT[3 * g * 64:(3 * g + 3) * 64, cs].rearrange("(h d) n -> d h n", h=3)
            for kg in range(T + 1):
                P.dma("sp", f"QA{kg}", w=[f"QA.q{kg}"], out=QA[0:64, kg, :, :], in_=qsrc)
            P.dma("sp", "QW", w=["QW.q"], out=QW[0:64, :, :], in_=qsrc)
            nm = 2 * T + 2
            for hi_ in range(3):
                h = 3 * g + hi_
                acc, acck = self.ps[4], self.psk[4]
                def mkA(m, T=T, g=g, hi_=hi_, acc=acc, acck=acck, nm=nm):
                    st = {}
                    c0 = 256 if m == 2 * T + 1 else 0
                    ncol = 512 - c0
                    masked = (m >= 2 * T)

                    def front():
                        sp_, spk = st_ring([5, 7])
                        st["sp"] = (sp_, spk)
                        P.op("pe", "matmul", r=[f"kcmpT.{g}.0", f"kcmpT.{g}.1", "kcmpT.0", "QA.q0"], w=[spk], out=sp_[:, 0:ncol],
                             lhsT=self.kcmpT[:, g, m * 128:(m + 1) * 128], rhs=QA[0:64, 0, hi_, c0:512], start=True, stop=(not masked))
                        if masked:
                            P.op("pe", "matmul", r=["ident_bf", "cmask"], w=[spk], out=sp_[:, 0:128], lhsT=self.ident_bf[:], rhs=cmask[:, 0, :], start=False, stop=False)
                            P.op("pe", "matmul", r=["ident_bf", "cmask"], w=[spk], out=sp_[:, 128:256], lhsT=self.ident_bf[:], rhs=cmask[:, 1, :], start=False, stop=True)

                    def back():
                        sp_, spk = st["sp"]
                        pt, ptk = self.pt()
                        P.op("act", "activation", r=[spk], w=[ptk], out=pt[:, 0:ncol], in_=sp_[:, 0:ncol], func=AF.Exp)
                        P.op("pe", "matmul", r=[ptk, "vcA.one", f"vcA.{g}.{m}"], w=[acck], out=acc[:, c0:512], lhsT=self.vcA[:, g, m, :], rhs=pt[:, 0:ncol],
                             start=(m == 0), stop=(m == nm - 1))
                        for jj in range(c0 // 128, 4):
                            lastm = 2 * T if jj < 2 else 2 * T + 1
                            P.op("pe", "matmul", r=[ptk, "ovA"], w=[self.psk[jj]], out=self.ps[jj][:, 0:257], lhsT=pt[:, jj * 128 - c0:(jj + 1) * 128 - c0],
                                 rhs=ovA[:, m, :], start=(m == 0), stop=(m == lastm))
                    return (front, back)
                _pipeline([mkA(m) for m in range(nm)], L=1)
                finish_branch(acc, acck, 0, h, hi_, T, True)
                for jj in range(4):
                    rq, rqk = rzq()
                    P.op("dve", "tensor_scalar", r=[self.psk[jj]], w=[rqk], out=rq[:], in0=self.ps[jj][:, 256:257], scalar1=1e-30, scalar2=None, op0=ALU.max)
                    P.op("dve", "reciprocal", r=[rqk], w=[rqk], out=rq[:], in_=rq[:])
                    if hi_ == 0:
                        P.op("dve", "tensor_scalar", r=[self.psk[jj], rqk], w=[f"impacc.{jj}"], out=impacc[:, jj, :], in0=self.ps[jj][:, 0:256], scalar1=rq[:, 0:1],
                             scalar2=None, op0=ALU.mult)
                    else:
                        P.op("dve", "scalar_tensor_tensor", r=[self.psk[jj], rqk, f"impacc.{jj}"], w=[f"impacc.{jj}"], out=impacc[:, jj, :], in0=self.ps[jj][:, 0:256],
                             scalar=rq[:, 0:1], in1=impacc[:, jj, :], op0=ALU.mult, op1=ALU.add)
            for jj in range(4):
                j = 4 * T + jj
                ad, adk = adjr()
                P.dma("sp", adk, w=[adk], out=ad[:], in_=adj_d[j])
                sc, sck = scr()
                P.op("dve", "tensor_tensor", r=[f"impacc.{jj}", adk], w=[sck], out=sc[:], in0=impacc[:, jj, :], in1=ad[:], op=ALU.add)
                mm, mk = m8()
                P.op("dve", "max", r=[sck], w=[mk + "a"], out=mm[:, 0:8], in_=sc[:])
                s2, s2k = scr()
                P.op("dve", "match_replace", r=[sck, mk + "a"], w=[s2k], out=s2[:], in_to_replace=mm[:, 0:8], in_values=sc[:], imm_value=-3.0e4)
                P.op("dve", "max", r=[s2k], w=[mk + "b"], out=mm[:, 8:16], in_=s2[:])
                P.op("dve", "tensor_scalar", r=[mk + "b"], w=[mk + "c"], out=mm[:, 15:16], in0=mm[:, 15:16], scalar1=-0.5 * BIGF, scalar2=None, op0=ALU.max)
                P.op("dve", "tensor_scalar", r=[sck, mk + "c", "SELT.z"], w=["SELT"], out=SELT[:, :, 64:128], in0=sc[:].rearrange("p (k r) -> p k r", k=4),
                     scalar1=mm[:, 15:16], scalar2=-1.0, op0=ALU.is_ge, op1=ALU.add)
                tp_key = self.psk[7]
                for kg in range(T + 1):
                    P.op("pe", "transpose", r=["SELT", "ident_bf"], w=[tp_key], out=tps[:, kg * 128:(kg + 1) * 128], in_=SELT[:, kg, :], identity=self.ident_bf[:])
                for hi_ in range(3):
                    eng = "act" if hi_ < 2 else "dve"
                    kw_ = dict(out=QA[64:128, 0:T + 1, hi_, jj * 128:(jj + 1) * 128], in_=tps[64:128, 0:(T + 1) * 128].rearrange("p (k q) -> p k q", k=T + 1))
                    if eng == "act":
                        P.op("act", "activation", r=[tp_key], w=[f"QA.s{hi_}"], func=AF.Copy, **kw_)
                    else:
                        P.op("dve", "tensor_copy", r=[tp_key], w=[f"QA.s{hi_}"], **kw_)
            loadedB = {}

            def mkB(J, hi_, kb, T=T, g=g, loadedB=loadedB):
                st = {}
                jstart = max(J, 4 * T)
                c0 = (jstart - 4 * T) * 128
                ncol = 512 - c0
                diag = (J >= 4 * T)
                kg = J // 4
                acc, acck = self.ps[hi_], self.psk[hi_]

                def front():
                    if J not in loadedB:
                        ks_, kk = KS()
                        P.dma("sp", kk, w=[kk + ".k"], out=ks_[0:64, :].rearrange("d (c p) -> d c p", c=8),
                              in_=ksT_all[:, g * 64:(g + 1) * 64, J * 128:(J + 1) * 128].rearrange("c d p -> d c p"))
                        P.dma("sp", kk + "e", w=[kk + ".e"], out=ks_[64:128, :], in_=epat_d[J % 4])
                        vs_, vk = VS()
                        P.dma("sp", vk, w=[vk], out=vs_[:, :, 0:64], in_=vs_all[:, J * 128:(J + 1) * 128, g * 64:(g + 1) * 64].rearrange("c p d -> p c d"))
                        loadedB[J] = (ks_, kk, vs_, vk)
                    ks_, kk, vs_, vk = loadedB[J]
                    sp_, spk = st_ring([3, 4, 5, 7])
                    st["sp"] = (sp_, spk)
                    P.op("pe", "matmul", r=[kk + ".k", kk + ".e", f"QA.q{kg}", f"QA.s{hi_}"], w=[spk], out=sp_[:, 0:ncol], lhsT=ks_[:, kb * 128:(kb + 1) * 128],
                         rhs=QA[:, kg, hi_, c0:512], start=True, stop=(not diag))
                    if diag:
                        P.op("pe", "matmul", r=["ident_bf", "madd"], w=[spk], out=sp_[:, 0:128], lhsT=self.ident_bf[:], rhs=madd[:, kb, :], start=False, stop=True)

                def back():
                    ks_, kk, vs_, vk = loadedB[J]
                    sp_, spk = st["sp"]
                    pt, ptk = self.pt()
                    P.op("act", "activation", r=[spk], w=[ptk], out=pt[:, 0:ncol], in_=sp_[:, 0:ncol], func=AF.Exp)
                    P.op("pe", "matmul", r=[ptk, vk, vk + ".one"], w=[acck], out=acc[:, c0:512], lhsT=vs_[:, kb, :], rhs=pt[:, 0:ncol],
                         start=(J == 0 and kb == 0), stop=(J == 4 * T + 3 and kb == 7))
                return (front, back)
            _pipeline([mkB(J, hi_, kb) for J in range(4 * T + 4) for hi_ in range(3) for kb in range(8)], L=2)
            for hi_ in range(3):
                finish_branch(self.ps[hi_], self.psk[hi_], 1, 3 * g + hi_, hi_, T, False)
            itemsC = []
            for jj in range(4):
                j = 4 * T + jj
                blocks = [r for r in range(4, 16) if not (j == 0 and r < 8)] if uniform_w else list(range(5))
                ldst = {}
                for bi, kb in enumerate(blocks):
                    def mkC(jj=jj, j=j, bi=bi, kb=kb, nblk=len(blocks), ldst=ldst, T=T, g=g):
                        st = {}

                        def front():
                            if "k" not in ldst:
                                kw_t, kwk = KW()
                                vw_t, vwk = VW()
                                if uniform_w:
                                    kwks, vwks = [], []
                                    for Jo, Jw in enumerate((j - 1, j)):
                                        if Jw < 0:
                                            continue
                                        P.dma("sp", kwk + str(Jo), w=[f"{kwk}.{Jo}"], out=kw_t[0:64, Jo * 1024:(Jo + 1) * 1024].rearrange("d (c p) -> d c p", c=8),
                                              in_=kwT_all[:, g * 64:(g + 1) * 64, Jw * 128:(Jw + 1) * 128].rearrange("c d p -> d c p"))
                                        P.dma("sp", vwk + str(Jo), w=[f"{vwk}.{Jo}"], out=vw_t[:, Jo * 8:(Jo + 1) * 8, 0:64],
                                              in_=vw_all[:, Jw * 128:(Jw + 1) * 128, g * 64:(g + 1) * 64].rearrange("c p d -> p c d"))
                                        kwks.append(f"{kwk}.{Jo}")
                                        vwks.append(f"{vwk}.{Jo}")
                                    krows = 64
                                else:
                                    P.dma("sp", kwk, w=[kwk], out=kw_t[:], in_=kwin_d[j, g])
                                    P.dma("sp", vwk, w=[vwk], out=vw_t[:, :, 0:64], in_=vwin_d[j, :, g * 64:(g + 1) * 64].rearrange("(b p) d -> p b d", p=128))
                                    kwks, vwks = [kwk], [vwk]
                                    krows = 65
                                ldst["k"] = (kw_t, kwk, vw_t, vwk, kwks, vwks, krows)
                            kw_t, kwk, vw_t, vwk, kwks, vwks, krows = ldst["k"]
                            sp_, spk = st_ring([3, 4, 5, 7])
                            st["sp"] = (sp_, spk)
                            if uniform_w:
                                wm, wmi = True, kb - 4
                            else:
                                wm, wmi = kb in (0, 4), (0 if kb == 0 else 1)
                            P.op("pe", "matmul", r=kwks + ["QW.q", "QW.one"], w=[spk], out=sp_[:, 0:384].rearrange("p (h q) -> p h q", h=3),
                                 lhsT=kw_t[0:krows, kb * 128:(kb + 1) * 128], rhs=QW[0:krows, :, jj * 128:(jj + 1) * 128], start=True, stop=(not wm))
                            if wm:
                                for hi_ in range(3):
                                    P.op("pe", "matmul", r=["ident_bf", "wmask"], w=[spk], out=sp_[:, hi_ * 128:(hi_ + 1) * 128], lhsT=self.ident_bf[:],
                                         rhs=wmask[:, wmi, :], start=False, stop=(hi_ == 2))

                        def back():
                            kw_t, kwk, vw_t, vwk, kwks, vwks, krows = ldst["k"]
                            sp_, spk = st["sp"]
                            pt, ptk = self.pt()
                            P.op("act", "activation", r=[spk], w=[ptk], out=pt[:, 0:384], in_=sp_[:, 0:384], func=AF.Exp)
                            for hi_ in range(3):
                                P.op("pe", "matmul", r=[ptk, vwk + ".one"] + vwks, w=[self.psk[hi_]], out=self.ps[hi_][:, jj * 128:(jj + 1) * 128], lhsT=vw_t[:, kb, :],
                                     rhs=pt[:, hi_ * 128:(hi_ + 1) * 128], start=(bi == 0), stop=(bi == nblk - 1))
                        return (front, back)
                    itemsC.append(mkC())
            _pipeline(itemsC, L=2)
            for hi_ in range(3):
                h = 3 * g + hi_
                finish_branch(self.ps[hi_], self.psk[hi_], 2, h, hi_, T, False)
                P.op("act", "activation", r=[f"ysum{hi_}"], w=[f"h{h // 2}.{T}"], out=self.hT[h // 2][(h % 2) * 64:(h % 2) * 64 + 64, cs], in_=ysum[hi_][:], func=AF.Copy)


KB.nsa_attn = _nsa_attn


def _dram_int(self, name, shape, dt=F32):
    return self.nc.dram_tensor(name, list(shape), dt, kind="Internal").ap()


def _all_gather(self, src2d, dst2d, name):
    self.P.dmaop("pool", "collective_compute", "cc_" + name, r=[], w=["cc." + name], inc=1, after_all_dma=True,
                 kind="AllGather", op=ALU.bypass, replica_groups=[list(range(8))], ins=[src2d.opt()], outs=[dst2d.opt()])


KB.dram_int = _dram_int
KB.all_gather = _all_gather


def host_wmask16(c):
    ps = np.arange(128)[:, None]
    pq = np.arange(128)[None, :]
    w = np.full((128, 12, 128), NEG, np.float32)
    for r in range(4, 16):
        diff = 8 + c - r
        if diff == 4:
            w[:, r - 4, :] = np.where(ps > pq, 0.0, NEG)
        elif diff in (1, 2, 3):
            w[:, r - 4, :] = 0.0
        elif diff == 0:
            w[:, r - 4, :] = np.where(ps <= pq, 0.0, NEG)
    return w.reshape(128, 12 * 128)


def host_halosel(c):
    m = np.zeros((128, 8, 32), np.float32)
    m[:, (c - 1) % 8, :] = 1.0
    return m.reshape(128, 256)


def _conv_halo_select(self, hall, halosel_d, ch, u):
    P = self.P
    if not hasattr(self, "_hsel"):
        self._hsel = self.sb("hsel", [128, 8, 32], F32)
        P.dma("sp", "hsel", w=["hsel"], out=self._hsel[:], in_=halosel_d.rearrange("p (c t) -> p c t", c=8))
        self._HC = self.ring("HC", 2, [128, 8, 16, 2], F32)
    hc, hk = self._HC()
    rows = slice(ch * 128, (ch + 1) * 128)
    P.dma("sp", hk, w=[hk + ".a"], out=hc[:, 0:7, :, :].rearrange("p c j t -> p c (j t)"), in_=hall[0:7, rows, :].rearrange("c p t -> p c t"))
    P.op("pool", "memset", w=[hk + ".z"], ap=hc[:, 7, 0:1, :], constant=0.0)
    P.dma("sp", hk + "b", w=[hk + ".b"], out=hc[:, 7, 1:16, :], in_=hall[7, rows, 0:30].rearrange("p (j t) -> p j t", t=2))
    hcf = hc[:].rearrange("p c j t -> p c (j t)")
    P.op("dve", "tensor_tensor", r=[hk + ".a", hk + ".b", hk + ".z", "hsel"], w=[hk + ".p"], out=hcf, in0=hcf, in1=self._hsel[:], op=ALU.mult)
    return hc, hk


KB.conv_halo_select = _conv_halo_select


import ml_dtypes
from concourse.bass_utils import run_bass_kernel_spmd

_PROJ_OUT = {
    "fox": [("qT", [768, NT], BF16), ("kT", [768, NT], BF16), ("v", [NT, 768], BF16), ("nlf", [128, 192], F32), ("qxT", [256, NT], BF16)],
    "conv": [("uT", [768, NT], F32), ("bgT", [768, NT], F32), ("qxT", [256, NT], BF16)],
    "nsa": [("qT", [768, NT], BF16), ("kcT", [256, NT], BF16), ("vcT", [256, NT], BF16), ("ksT", [256, NT], BF16), ("vs", [NT, 256], BF16),
            ("kwT", [256, NT], BF16), ("vw", [NT, 256], BF16), ("sgT", [36, NT], F32), ("qxT", [256, NT], BF16)],
}


def _ffn_block(k, tag):
    g = k.dram_in(f"g_{tag}", [128, 8]); wi = k.dram_in(f"wi_{tag}", [D, 2 * DFF]); wo = k.dram_in(f"wo_{tag}", [DFF, D])
    with k.scope():
        k.ffn_setup()
        k.rmsnorm(g)
        k.ffn(wi, wo)


def _proj_block(k, kind):
    gm = k.dram_in("gm", [128, 8])
    outs = {n: k.dram_out("o_" + n, shp, dt) for (n, shp, dt) in _PROJ_OUT[kind]}
    with k.scope():
        k.proj_setup()
        k.rmsnorm(gm)
        if kind == "fox":
            w = k.dram_in("wproj", [D, 2572]); bf = k.dram_in("bf", [12])
            k.fox_proj(w, bf, outs["qT"], outs["kT"], outs["v"], outs["nlf"], outs["qxT"])
        elif kind == "conv":
            w = k.dram_in("wproj", [D, 2560])
            k.conv_proj(w, outs["uT"], outs["bgT"], outs["qxT"])
        else:
            w = k.dram_in("wproj", [D, 2596]); wsw = k.dram_in("wsw", [D, 1536]); pos = k.dram_in("pos", [NT]); ropec = k.dram_in("ropec", [128, 2])
            k.nsa_proj(w, wsw, pos, ropec, outs["qT"], outs["kcT"], outs["vcT"], outs["ksT"], outs["vs"], outs["kwT"], outs["vw"], outs["sgT"], outs["qxT"])


def _mix_block(k, kind):
    qxT = k.dram_in("qxT", [256, NT], BF16)
    memT = k.dram_in("memT", [8, 128, 256]); gmem = k.dram_in("gmem", [128, 8]); wkv = k.dram_in("wkv", [D, 512]); wmo = k.dram_in("wmo", [D, D])
    if kind == "fox":
        qT = k.dram_in("qT", [768, NT], BF16); kT = k.dram_in("kT_all", [8, 768, NT], BF16); v = k.dram_in("v_all", [8, NT, 768], BF16)
        nlf = k.dram_in("nlf_all", [8, 128, 192], F32); madd = k.dram_in("maskadd", [128, 1024]); selm = k.dram_in("selmask", [128, 1536])
        with k.scope():
            k.attn_consts()
            k.fox_attn(qT, kT, v, nlf, madd, selm)
    elif kind == "conv":
        uT = k.dram_in("uT", [768, NT]); bgT = k.dram_in("bgT", [768, NT]); halo = k.dram_in("halo", [768, 16, 2]); cw = k.dram_in("cw", [128, 18])
        with k.scope():
            k.conv_mix(uT, halo, bgT, cw)
    else:
        qT = k.dram_in("qT", [768, NT], BF16)
        kcT = k.dram_in("kcT_all", [8, 256, NT], BF16); vcT = k.dram_in("vcT_all", [8, 256, NT], BF16)
        ksT = k.dram_in("ksT_all", [8, 256, NT], BF16); vs = k.dram_in("vs_all", [8, NT, 256], BF16)
        kwTa = k.dram_in("kwT_all", [8, 256, NT], BF16); vwa = k.dram_in("vw_all", [8, NT, 256], BF16); wm16 = k.dram_in("wmask16", [128, 12 * 128])
        sgT = k.dram_in("sgT", [36, NT], F32)
        adj = k.dram_in("adj", [16, 128, 256]); cmask = k.dram_in("cmask", [128, 256]); madd = k.dram_in("maskadd", [128, 1024])
        ovA = k.dram_in("ovA", [128, 8 * 257]); gsel = k.dram_in("gsel", [100, 36 * 128]); epat = k.dram_in("epat", [4, 64, 1024], BF16)
        cpos = k.dram_in("cpos", [2, 64, 32]); cw1 = k.dram_in("cw1", [2, 2048, 64]); cw2 = k.dram_in("cw2", [2, 64, 64])
        with k.scope():
            k.attn_consts()
            k.nsa_compress(kcT, vcT, cpos, cw1, cw2)
            k.nsa_attn(qT, ksT, vs, None, None, sgT, adj, cmask, madd, ovA, gsel, epat, None, kwT_all=kwTa, vw_all=vwa, wmask16_d=wm16)
    with k.scope():
        k.norm_setup()
        k.attn_consts()
        k.mem_attn(memT, gmem, wkv, qxT)
    with k.scope():
        k.mix_out(wmo)


def build_launch(li, kinds):
    k = KB()
    x_in = k.dram_in("xT_in", [8, 128, NT])
    x_out = k.dram_out("xT_out", [8, 128, NT])
    k.setup_common()
    k.load_xT(x_in)
    if li > 0:
        _mix_block(k, kinds[li - 1])
        _ffn_block(k, "b")
    if li < 4:
        _ffn_block(k, "a")
        k.store_xT(x_out)
        _proj_block(k, kinds[li])
    else:
        gf = k.dram_in("gfinal", [128, 8])
        with k.scope():
            k.norm_setup()
            k.rmsnorm(gf, final=True)
        k.store_xT(x_out)
    return k.finish()


_KINDS = ["fox", "conv", "nsa", "fox"]


def kernel(x, mem, ffn1_norm, ffn1_w_in, ffn1_w_out, mix_norm, mix_w_out, mem_norm, mem_w_kv,
           fox_w_in, fox_b_f, conv_w_in, conv_w, nsa_w_in, nsa_cmp_pos, nsa_cmp_w1, nsa_cmp_w2,
           ffn2_norm, ffn2_w_in, ffn2_w_out, final_norm):
    f32 = lambda a: np.ascontiguousarray(np.asarray(a, dtype=np.float32))
    gl = lambda a: np.ascontiguousarray(f32(a).reshape(8, 128).T)
    x = f32(x)[0]
    memT = np.ascontiguousarray(f32(mem)[0].T.reshape(8, 128, 256))
    cores = list(range(8))
    xs = [np.ascontiguousarray(x[local_tokens(c)].T.reshape(8, 128, NT)) for c in cores]
    proj = None
    for li in range(5):
        nc = build_launch(li, _KINDS)
        maps = []
        for c in cores:
            m = {"xT_in": xs[c]}
            if li > 0:
                i = li - 1
                kind = _KINDS[i]
                m.update({"qxT": proj[c]["o_qxT"], "memT": memT, "gmem": gl(mem_norm), "wkv": f32(mem_w_kv[i]), "wmo": f32(mix_w_out[i]),
                          "g_b": gl(ffn2_norm[i]), "wi_b": f32(ffn2_w_in[i]), "wo_b": f32(ffn2_w_out[i])})
                if kind == "fox":
                    m.update({"qT": proj[c]["o_qT"], "kT_all": st["kT"], "v_all": st["v"], "nlf_all": st["nlf"],
                              "maskadd": host_maskadd(c).reshape(128, 1024), "selmask": host_selmask(c)})
                elif kind == "conv":
                    m.update({"uT": proj[c]["o_uT"], "bgT": proj[c]["o_bgT"], "halo": st["halo"][c], "cw": np.ascontiguousarray(f32(conv_w[0]).reshape(3, 6, 128).transpose(2, 0, 1).reshape(128, 18))})
                else:
                    m.update({"qT": proj[c]["o_qT"], "kcT_all": st["kcT"], "vcT_all": st["vcT"], "ksT_all": st["ksT"], "vs_all": st["vs"],
                              "kwT_all": st["kwT"], "vw_all": st["vw"], "wmask16": host_wmask16(c), "sgT": proj[c]["o_sgT"], "adj": host_adj(c), "cmask": host_cmask(c),
                              "maskadd": host_maskadd(c).reshape(128, 1024), "ovA": st["ovA"], "gsel": st["gsel"], "epat": st["epat"],
                              "cpos": np.ascontiguousarray(f32(nsa_cmp_pos[0]).transpose(0, 2, 1)), "cw1": f32(nsa_cmp_w1[0]), "cw2": f32(nsa_cmp_w2[0])})
            if li < 4:
                kind = _KINDS[li]
                m.update({"g_a": gl(ffn1_norm[li]), "wi_a": f32(ffn1_w_in[li]), "wo_a": f32(ffn1_w_out[li]), "gm": gl(mix_norm[li])})
                if kind == "fox":
                    jj = li // 3
                    m.update({"wproj": f32(fox_w_in[jj]), "bf": f32(fox_b_f[jj])})
                elif kind == "conv":
                    m.update({"wproj": f32(conv_w_in[0])})
                else:
                    W = f32(nsa_w_in[0])
                    wsw = np.concatenate([host_swap_cols(W[:, 0:768], 768), host_swap_cols(W[:, 768:1024], 256),
                                          host_swap_cols(W[:, 1280:1536], 256), host_swap_cols(W[:, 1792:2048], 256)], 1)
                    m.update({"wproj": W, "wsw": np.ascontiguousarray(wsw), "pos": local_tokens(c).astype(np.float32), "ropec": host_rope_cols()})
            else:
                m.update({"gfinal": gl(final_norm)})
            maps.append(m)
        res = run_bass_kernel_spmd(nc, maps, core_ids=cores).results
        xs = [res[c]["xT_out"] for c in cores]
        if li < 4:
            proj = res
            kind = _KINDS[li]
            stack = lambda n: np.stack([res[c]["o_" + n] for c in cores])
            st = {}
            if kind == "fox":
                st = {"kT": stack("kT"), "v": stack("v"), "nlf": stack("nlf")}
            elif kind == "conv":
                uall = [res[c]["o_uT"] for c in cores]
                halos = []
                for c in cores:
                    h = np.zeros((768, 16, 2), np.float32)
                    for j in range(16):
                        gb = 8 * j + c - 1
                        if gb >= 0:
                            h[:, j, :] = uall[gb % 8][:, (gb // 8) * 128 + 126:(gb // 8) * 128 + 128]
                    halos.append(h)
                st = {"halo": halos}
            else:
                st = {"kcT": stack("kcT"), "vcT": stack("vcT"), "ksT": stack("ksT"), "vs": stack("vs"), "kwT": stack("kwT"), "vw": stack("vw"),
                      "ovA": host_ovA(), "gsel": host_gsel(), "epat": host_epat().astype(ml_dtypes.bfloat16)}
    out = np.zeros((S, D), np.float32)
    for c in cores:
        out[local_tokens(c)] = xs[c].reshape(D, NT).T
    return out[None]
```
